# Optimizing a Trainium2 kernel written in Bass

```python
import jax, jax.numpy as jnp
from jax import lax
import numpy as np

D_MODEL = 1024
BATCH = 2
SEQ = 16384
DEPTH = 1
DEC_BATCH = 8
DEC_SEQ = 32
PAST_LEN = 4096

CHUNK = 64
WINDOW = 128
WIN_CHUNKS = WINDOW // CHUNK
BAND = (WIN_CHUNKS + 1) * CHUNK
HEAD_DIM = 64
ATTN_WIDTH = D_MODEL // 2
N_Q_HEADS = ATTN_WIDTH // HEAD_DIM
N_KV_HEADS = 2
GROUP = N_Q_HEADS // N_KV_HEADS
GLA_WIDTH = D_MODEL - ATTN_WIDTH
GLA_HEADS = 4
GLA_DV = GLA_WIDTH // GLA_HEADS
GLA_DK = GLA_DV // 2
GLA_LOWRANK = 16
GLA_TAU = 16.0
D_FF = 4 * D_MODEL
EPS = 1e-6

IN_SIZES = (N_Q_HEADS * HEAD_DIM,
            N_KV_HEADS * HEAD_DIM,
            N_KV_HEADS * HEAD_DIM,
            GLA_HEADS * GLA_DK,
            GLA_HEADS * GLA_DK,
            GLA_HEADS * GLA_DV,
            GLA_WIDTH,
            GLA_LOWRANK)
N_IN = sum(IN_SIZES)
IN_SPLITS = [sum(IN_SIZES[:i + 1]) for i in range(len(IN_SIZES) - 1)]

kernel_name = "hybrid_swa_sink_gla_stream_step"


def rmsnorm(x, g):
    x32 = x.astype(jnp.float32)
    ms = jnp.mean(x32 * x32, axis=-1, keepdims=True)
    return (x32 * lax.rsqrt(ms + EPS) * g.astype(jnp.float32)).astype(x.dtype)


def mixer_inputs(h, w_in, w_alpha, b_alpha, g_q, g_k):
    B, T, _ = h.shape
    z = h @ w_in
    qa, ka, va, qg, kg, vg, rg, ca = jnp.split(z, IN_SPLITS, axis=-1)
    qa = rmsnorm(qa.reshape(B, T, N_KV_HEADS, GROUP, HEAD_DIM), g_q)
    ka = rmsnorm(ka.reshape(B, T, N_KV_HEADS, HEAD_DIM), g_k)
    va = va.reshape(B, T, N_KV_HEADS, HEAD_DIM)
    qg = qg.reshape(B, T, GLA_HEADS, GLA_DK) * (GLA_DK ** -0.5)
    kg = kg.reshape(B, T, GLA_HEADS, GLA_DK)
    vg = vg.reshape(B, T, GLA_HEADS, GLA_DV)
    log_a = jax.nn.log_sigmoid((ca @ w_alpha + b_alpha).astype(jnp.float32)) / GLA_TAU
    log_a = log_a.reshape(B, T, GLA_HEADS, GLA_DK)
    return qa, ka, va, qg, kg, vg, rg, log_a


def sink_attend(q, k, v, sinks, mask):
    s = jnp.einsum('...qhgd,...khd->...hgqk', q.astype(jnp.float32), k.astype(jnp.float32)) * (HEAD_DIM ** -0.5)
    if mask is not None:
        s = jnp.where(mask, s, -jnp.inf)
    sink = sinks.astype(jnp.float32).reshape(N_KV_HEADS, GROUP)[:, :, None, None]
    m = jnp.maximum(jnp.max(s, axis=-1, keepdims=True), sink)
    p = jnp.exp(s - m)
    denom = jnp.sum(p, axis=-1, keepdims=True) + jnp.exp(sink - m)
    return jnp.einsum('...hgqk,...khd->...qhgd', p / denom, v.astype(jnp.float32))


def gla_chunked(q, k, v, log_a, s0):
    B, nc, L = q.shape[:3]
    causal = jnp.tril(jnp.ones((L, L), dtype=bool))[None, :, :, None, None]
    xs = tuple(jnp.moveaxis(a.astype(jnp.float32), 1, 0) for a in (q, k, v, log_a))

    def step(S, inp):
        qc, kc, vc, gc = inp
        b = jnp.cumsum(gc, axis=1)
        o_inter = jnp.einsum('blhk,bhkv->blhv', qc * jnp.exp(b), S)
        diff = b[:, :, None] - b[:, None, :]
        decay = jnp.exp(jnp.where(causal, diff, -jnp.inf))
        A = jnp.einsum('btshk,bshk->bhts', qc[:, :, None] * decay, kc)
        o_intra = jnp.einsum('bhts,bshv->bthv', A, vc)
        bL = b[:, -1]
        k_dec = kc * jnp.exp(bL[:, None] - b)
        S_new = jnp.exp(bL)[..., None] * S + jnp.einsum('bshk,bshv->bhkv', k_dec, vc)
        return S_new, o_inter + o_intra

    s_final, o = lax.scan(step, s0, xs)
    o = jnp.moveaxis(o, 0, 1).reshape(B, nc * L, GLA_HEADS, GLA_DV)
    return o, s_final


def mixer_output(attn_o, gla_o, rg, g_gla_out, w_out, dtype):
    B, T = attn_o.shape[:2]
    a = attn_o.reshape(B, T, ATTN_WIDTH).astype(dtype)
    g = (rmsnorm(gla_o, g_gla_out).reshape(B, T, GLA_WIDTH) * jax.nn.silu(rg.astype(jnp.float32))).astype(dtype)
    return jnp.concatenate([a, g], axis=-1) @ w_out


def ffn(x, g, w_up, w_down):
    h = rmsnorm(x, g) @ w_up
    return jnp.square(jax.nn.relu(h)) @ w_down


def prompt_layer(x, p):
    g_mix, w_in, w_alpha, b_alpha, g_q, g_k, sinks, g_gla_out, w_out, g_ffn, w_up, w_down = p
    B, T, _ = x.shape
    nc = T // CHUNK
    qa, ka, va, qg, kg, vg, rg, log_a = mixer_inputs(rmsnorm(x, g_mix), w_in, w_alpha, b_alpha, g_q, g_k)
    pad = WIN_CHUNKS * CHUNK
    kc = jnp.pad(ka, ((0, 0), (pad, 0), (0, 0), (0, 0))).reshape(B, nc + WIN_CHUNKS, CHUNK, N_KV_HEADS, HEAD_DIM)
    vc = jnp.pad(va, ((0, 0), (pad, 0), (0, 0), (0, 0))).reshape(B, nc + WIN_CHUNKS, CHUNK, N_KV_HEADS, HEAD_DIM)
    k_band = jnp.concatenate([kc[:, j:j + nc] for j in range(WIN_CHUNKS + 1)], axis=2)
    v_band = jnp.concatenate([vc[:, j:j + nc] for j in range(WIN_CHUNKS + 1)], axis=2)
    kpos = jnp.arange(nc)[:, None] * CHUNK + jnp.arange(BAND)[None, :] - pad
    mask = (kpos >= 0)[:, None, None, None, :]
    q_blk = qa.reshape(B, nc, CHUNK, N_KV_HEADS, GROUP, HEAD_DIM)
    attn_o = sink_attend(q_blk, k_band, v_band, sinks, mask).reshape(B, T, N_Q_HEADS, HEAD_DIM)
    to_chunks = lambda a: a.reshape(B, nc, CHUNK, *a.shape[2:])
    s0 = jnp.zeros((B, GLA_HEADS, GLA_DK, GLA_DV), jnp.float32)
    gla_o, s_final = gla_chunked(to_chunks(qg), to_chunks(kg), to_chunks(vg), to_chunks(log_a), s0)
    h = x + mixer_output(attn_o, gla_o, rg, g_gla_out, w_out, x.dtype)
    y = h + ffn(h, g_ffn, w_up, w_down)
    return y, ka[:, T - WINDOW:], va[:, T - WINDOW:], s_final


def sample_layer(x, cache_k, cache_v, state, p):
    g_mix, w_in, w_alpha, b_alpha, g_q, g_k, sinks, g_gla_out, w_out, g_ffn, w_up, w_down = p
    B, T, _ = x.shape
    qa, ka, va, qg, kg, vg, rg, log_a = mixer_inputs(rmsnorm(x, g_mix), w_in, w_alpha, b_alpha, g_q, g_k)
    k_all = jnp.concatenate([cache_k.astype(ka.dtype), ka], axis=1)
    v_all = jnp.concatenate([cache_v.astype(va.dtype), va], axis=1)
    attn_o = sink_attend(qa, k_all, v_all, sinks, None).reshape(B, T, N_Q_HEADS, HEAD_DIM)
    one = lambda a: a.reshape(B, 1, T, *a.shape[2:])
    gla_o, s_new = gla_chunked(one(qg), one(kg), one(vg), one(log_a), state.astype(jnp.float32))
    h = x + mixer_output(attn_o, gla_o, rg, g_gla_out, w_out, x.dtype)
    y = h + ffn(h, g_ffn, w_up, w_down)
    return y, ka, va, s_new


def setup_inputs(seed: int = 0) -> dict:
    key = jax.random.key(seed)
    ks = jax.random.split(key, 20)
    f32 = jnp.float32
    nrm = lambda k, shape, s=1.0: jax.random.normal(k, shape, f32) * s
    return {
        "x_prompt": nrm(ks[0], (BATCH, SEQ, D_MODEL)),
        "x_sample": nrm(ks[1], (DEC_BATCH, DEC_SEQ, D_MODEL)),
        "cache_k": nrm(ks[2], (DEPTH, DEC_BATCH, WINDOW, N_KV_HEADS, HEAD_DIM)),
        "cache_v": nrm(ks[3], (DEPTH, DEC_BATCH, WINDOW, N_KV_HEADS, HEAD_DIM)),
        "state_gla": nrm(ks[4], (DEPTH, DEC_BATCH, GLA_HEADS, GLA_DK, GLA_DV)),
        "g_mix": 1.0 + nrm(ks[5], (DEPTH, D_MODEL), 0.02),
        "w_in": nrm(ks[6], (DEPTH, D_MODEL, N_IN), D_MODEL ** -0.5),
        "w_alpha": nrm(ks[7], (DEPTH, GLA_LOWRANK, GLA_HEADS * GLA_DK), GLA_LOWRANK ** -0.5),
        "b_alpha": nrm(ks[8], (DEPTH, GLA_HEADS * GLA_DK), 0.1),
        "g_q": 1.0 + nrm(ks[9], (DEPTH, HEAD_DIM), 0.02),
        "g_k": 1.0 + nrm(ks[10], (DEPTH, HEAD_DIM), 0.02),
        "sinks": nrm(ks[11], (DEPTH, N_Q_HEADS), 0.5),
        "g_gla_out": 1.0 + nrm(ks[12], (DEPTH, GLA_DV), 0.02),
        "w_out": nrm(ks[13], (DEPTH, D_MODEL, D_MODEL), D_MODEL ** -0.5),
        "g_ffn": 1.0 + nrm(ks[14], (DEPTH, D_MODEL), 0.02),
        "w_up": nrm(ks[15], (DEPTH, D_MODEL, D_FF), D_MODEL ** -0.5),
        "w_down": nrm(ks[16], (DEPTH, D_FF, D_MODEL), D_FF ** -0.5),
    }


def reference(x_prompt, x_sample, cache_k, cache_v, state_gla, g_mix, w_in, w_alpha, b_alpha,
              g_q, g_k, sinks, g_gla_out, w_out, g_ffn, w_up, w_down):
    params = (g_mix, w_in, w_alpha, b_alpha, g_q, g_k, sinks, g_gla_out, w_out, g_ffn, w_up, w_down)
    yp, ys = x_prompt, x_sample
    kp_l, vp_l, sp_l, ks_l, vs_l, ss_l = [], [], [], [], [], []
    for l in range(DEPTH):
        p = tuple(w[l] for w in params)
        yp, kp, vp, sp = prompt_layer(yp, p)
        ys, k_s, v_s, s_s = sample_layer(ys, cache_k[l], cache_v[l], state_gla[l], p)
        kp_l.append(kp); vp_l.append(vp); sp_l.append(sp)
        ks_l.append(k_s); vs_l.append(v_s); ss_l.append(s_s)
    return (yp, ys, jnp.stack(kp_l), jnp.stack(vp_l), jnp.stack(sp_l),
            jnp.stack(ks_l), jnp.stack(vs_l), jnp.stack(ss_l))
```

```python
import os
import numpy as np
from contextlib import ExitStack
import concourse.bass as bass
import concourse.mybir as mybir
from concourse.bass_utils import run_bass_kernel_spmd

F32 = mybir.dt.float32
BF16 = mybir.dt.bfloat16
AF = mybir.ActivationFunctionType
ALU = mybir.AluOpType

D = 1024
SEG = 4096
MT = 512
NPRE = 24
NS = 14
EPS = 1e-6
SAME_ENGINE_SYNC = os.environ.get("NOSES", "") == ""

A_QA, A_KA, A_QG, A_KG, A_RG, A_CA = 0, 4, 6, 8, 10, 14
B0, O0, U0, D0 = 15, 23, 31, 63
NSLAB = 95
SERIAL = os.environ.get("SERIAL", "") != ""


class StopTile(Exception):
    pass


class Res:
    __slots__ = ("name", "w", "r")

    def __init__(self, name):
        self.name = name
        self.w = None
        self.r = []


class Eng:
    def __init__(self, name):
        self.name = name
        self.is_pe = name == "pe"
        self.items = []
        self.n = 0
        self.needed = set()
        self.seen = {}


class DmaGroup:
    def __init__(self, name):
        self.name = name
        self.count = 0
        self.sem = None


class Prog:
    def __init__(self):
        self.engs = {k: Eng(k) for k in ("pe", "act", "dve", "pool", "sp")}
        self.groups = []

    def group(self, name):
        g = DmaGroup(name)
        self.groups.append(g)
        return g

    def _deps(self, eng, reads, writes):
        deps = []
        for r in reads:
            if r.w is not None:
                deps.append(r.w)
        for w in writes:
            if w.w is not None:
                deps.append(w.w)
            deps.extend(w.r)
        out = {}
        for t in deps:
            kind, src, idx = t
            if kind == "e" and src is eng and (eng.is_pe or not SAME_ENGINE_SYNC):
                continue
            key = (kind, id(src))
            if eng.seen.get(key, 0) >= idx:
                continue
            if key not in out or out[key][2] < idx:
                out[key] = t
        for key, t in out.items():
            eng.seen[key] = t[2]
            if t[0] == "e":
                t[1].needed.add(t[2])
        return list(out.values())

    def _serial(self, eng, waits):
        lt = getattr(self, "last_tok", None)
        if SERIAL and lt is not None and not (lt[0] == "e" and lt[1] is eng and eng.is_pe):
            key = (lt[0], id(lt[1]))
            if eng.seen.get(key, 0) < lt[2]:
                eng.seen[key] = lt[2]
                if lt[0] == "e":
                    lt[1].needed.add(lt[2])
                waits = [w for w in waits if (w[0], id(w[1])) != key] + [lt]
        return waits

    def _finish(self, tok, reads, writes):
        self.last_tok = tok
        for r in reads:
            if len(r.r) > 64:
                r.r = r.r[-64:] if r.w is None else r.r
            r.r.append(tok)
        for w in writes:
            w.w = tok
            w.r = []
        return tok

    def op(self, engname, fn, reads=(), writes=()):
        eng = self.engs[engname]
        waits = self._serial(eng, self._deps(eng, reads, writes))
        eng.n += 1
        tok = ("e", eng, eng.n)
        eng.items.append((waits, fn, eng.n, None))
        return self._finish(tok, reads, writes)

    def dma(self, engname, group, fn, reads=(), writes=()):
        eng = self.engs[engname]
        waits = self._serial(eng, self._deps(eng, reads, writes))
        if group.count > 0 and not getattr(group, "multi", False):
            key = ("d", id(group))
            if eng.seen.get(key, 0) < group.count:
                eng.seen[key] = group.count
                waits = [w for w in waits if (w[0], id(w[1])) != key] + [("d", group, group.count)]
        eng.n += 1
        group.count += 1
        tok = ("d", group, group.count)
        eng.items.append((waits, fn, eng.n, group))
        return self._finish(tok, reads, writes)

    def wait_all(self, engname, toks):
        eng = self.engs[engname]
        waits = []
        for t in toks:
            if t is None:
                continue
            if t[0] == "e":
                t[1].needed.add(t[2])
            waits.append(t)
        eng.items.append((waits, None, None, None))

    def emit(self, block, sems):
        handles = {"pe": "tensor", "act": "scalar", "dve": "vector", "pool": "gpsimd", "sp": "sync"}
        for e in self.engs.values():
            e.sigcount = {}
            c = 0
            for (_, fn, idx, grp) in e.items:
                if idx is not None and grp is None and idx in e.needed:
                    c += 1
                    e.sigcount[idx] = c
        prog = self

        def make(ename):
            e = prog.engs[ename]

            def body(h):
                for (waits, fn, idx, grp) in e.items:
                    for t in waits:
                        if t[0] == "e":
                            h.wait_ge(sems[t[1].name], t[1].sigcount[t[2]])
                        else:
                            h.wait_ge(t[1].sem, 16 * t[2])
                    if fn is None:
                        continue
                    ins = fn(h)
                    if grp is not None:
                        ins.then_inc(grp.sem, 16)
                    elif idx in e.needed:
                        ins.then_inc(sems[ename], 1)
            return body

        for ename, attr in handles.items():
            if prog.engs[ename].items:
                getattr(block, attr)(make(ename))


def build_program(npre=NPRE, nmain=SEG // MT, ntiles=None):
    SEG = nmain * MT
    nc = bass.Bass("TRN2", target_bir_lowering=False)
    P = Prog()
    es = ExitStack()

    def din(name, shape):
        return nc.dram_tensor(name, list(shape), F32, kind="ExternalInput").ap()

    def dout(name, shape):
        return nc.dram_tensor(name, list(shape), F32, kind="ExternalOutput").ap()

    NX = npre * MT + SEG
    xin = din("xin", [NX, D])
    xs_d = din("xs", [32, D])
    ck_d = din("ck", [128, 128])
    cv_d = din("cv", [128, 128])
    st_d = din("st", [4, 64, 128])
    hb_d = din("hb", [128, 1])
    g_mix_d = din("g_mix", [D])
    w_in_d = din("w_in", [D, 2320])
    w_alpha_d = din("w_alpha", [16, 256])
    b_alpha_d = din("b_alpha", [1, 256])
    g_q_d = din("g_q", [64, 1])
    g_k_d = din("g_k", [64, 1])
    sinks_d = din("sinks", [1, 8])
    g_gla_d = din("g_gla", [128, 1])
    w_out_d = din("w_out", [D, D])
    g_ffn_d = din("g_ffn", [D])
    w_up_d = din("w_up", [D, 4096])
    w_down_d = din("w_down", [4096, D])

    y_d = dout("y", [SEG, D])
    ys_d = dout("ys", [32, D])
    kp_d = dout("kp", [128, 128])
    vp_d = dout("vp", [128, 128])
    sp_d = dout("sp", [4, 64, 128])
    ksam_d = dout("ksam", [32, 128])
    vsam_d = dout("vsam", [32, 128])
    ssam_d = dout("ssam", [4, 64, 128])

    wstream = nc.dram_tensor("wstream", [NSLAB, 128, 1024], BF16).ap()

    def sb(name, shape, dt):
        return es.enter_context(nc.sbuf_tensor(name, list(shape), dt))

    def sem(name):
        return es.enter_context(nc.semaphore(name))

    xts = [sb(f"xt{i}", [128, 4, D], F32) for i in range(2)]
    ssq = sb("ssq", [128, 8], F32)
    rstd = sb("rstd", [128, 8], F32)
    nb = [sb(f"nb{i}", [128, D], BF16) for i in range(4)]
    nTa = sb("nTa", [128, 8, MT], BF16)
    nTb = sb("nTb", [128, 8, MT], BF16)
    qaT = sb("qaT", [128, 4, MT], BF16)
    KTz = [[sb(f"KT{i}{p}", [128, 128 + MT], BF16) for p in range(2)] for i in range(2)]
    k32T = sb("k32T", [128, 128], F32)
    qgT = sb("qgT", [128, 2, MT], F32)
    kgT = sb("kgT", [128, 2, MT], F32)
    sg = sb("sg", [128, 4, MT], F32)
    caT = sb("caT", [17, MT], BF16)
    Vlo = sb("Vlo", [128, 5, 2, 128], BF16)
    Vhi = sb("Vhi", [128, 5, 2, 128], BF16)
    kgTok = sb("kgTok", [128, 4, 256], F32)
    vgTok = sb("vgTok", [128, 4, 512], BF16)
    vouts = [sb(f"vout32_{i}", [128, 128], F32) for i in range(2)]
    kouts = [sb(f"kout32_{i}", [128, 128], F32) for i in range(2)]
    sqb = sb("sqb", [128, MT], BF16)
    rtmp = sb("rtmp", [128, MT], F32)
    PT = [sb(f"PT{i}", [128, 2, 2, 512], BF16) for i in range(1)]
    dn = sb("dn", [128, 512], F32)
    mixT = sb("mixT", [128, 8, MT], BF16)
    eu = sb("eu", [128, 1024], F32)
    Lh = sb("Lh", [128, 4, 256], BF16)
    Ll = sb("Ll", [128, 4, 256], BF16)
    E1 = sb("E1", [128, 256], F32)
    E2 = sb("E2", [128, 256], F32)
    qtT = sb("qtT", [128, 2, 128], BF16)
    ktTz = [sb(f"ktT{p}", [128, 2, 128], BF16) for p in range(2)]
    erev = eu
    kdecP = [sb(f"kdec{p}", [128, 4, 256], BF16) for p in range(2)]
    AT = sb("AT", [128, 4, 128], BF16)
    S = sb("S", [128, 2, 128], F32)
    Sbz = [sb(f"Sb{p}", [128, 2, 128], BF16) for p in range(2)]
    dec = sb("dec", [128, 8], F32)
    otmp = dn
    actT = [sb(f"actT{i}", [128, 8, MT], BF16) for i in range(2)]
    relu_t = [sb(f"relu{i}", [128, MT], F32) for i in range(2)]
    ring = sb("ring", [128, NS, 1024], BF16)
    stg_in = [sb(f"stgi{i}", [128, 1024], F32) for i in range(3)]
    stg_out = [sb(f"stgo{i}", [128, 1024], BF16) for i in range(3)]
    gm = sb("gm", [128, 8], F32)
    gf = sb("gf", [128, 8], F32)
    gq8 = sb("gq8", [128, 1], F32)
    gk1 = sb("gk1", [128, 1], F32)
    ggla = sb("ggla", [128, 1], F32)
    esink = sb("esink", [128, 4], F32)
    hb = sb("hbt", [128, 1], F32)
    wal32 = sb("wal32", [17, 256], F32)
    wal = sb("wal", [17, 256], BF16)
    triI = sb("triI", [128, 128], BF16)
    triR = sb("triR", [128, 128], BF16)
    caus = sb("caus", [128, 128], F32)
    negcol = sb("negcol", [128, 1], BF16)
    blk1 = sb("blk1", [128, 128], BF16)
    all1 = sb("all1", [128, 128], BF16)
    oneslo = sb("oneslo", [128, 128], BF16)
    oneshi = sb("oneshi", [128, 128], BF16)
    identb = sb("identb", [128, 128], BF16)
    identf = sb("identf", [128, 128], F32)
    ckd = sb("ckd", [128, 2, 128], F32)
    cv32 = sb("cv32", [128, 128], F32)

    psum_all = es.enter_context(nc.psum_tensor("psum_all", [128, 8, 512], F32))
    banks = [psum_all[:, i, :] for i in range(8)]
    U4 = psum_all[:, 6:8, :].rearrange("p a n -> p (a n)")
    RB = [Res(f"bank{i}") for i in range(8)]

    sems = {k: sem("s_" + k) for k in ("pe", "act", "dve", "pool", "sp")}

    def grp(name):
        g = P.group(name)
        g.sem = sem("g_" + name)
        return g

    g_misc = grp("misc")
    g_misc.multi = True
    g_x = [grp(f"x{i}") for i in range(4)]
    g_y = [grp(f"y{i}") for i in range(4)]
    g_out = grp("out")
    g_out.multi = True
    g_si = [grp(f"si{i}") for i in range(3)]
    for g_ in g_si:
        g_.multi = True
    g_so = [grp(f"so{i}") for i in range(3)]
    g_ring = [grp(f"r{i}") for i in range(NS)]

    block = es.enter_context(nc.Block())

    R = {}

    def res(name):
        if name not in R:
            R[name] = Res(name)
        return R[name]

    R_xts = [[res(f"xt{b}_{s}") for s in range(4)] for b in range(2)]
    R_nTa = [res(f"nTa{s}") for s in range(4)]
    R_nTb = [res(f"nTb{s}") for s in range(4)]
    R_ring = [res(f"ring{i}") for i in range(NS)]
    R_const = res("const")

    def OP(eng, fn, reads=(), writes=()):
        return P.op(eng, fn, list(reads), list(writes))

    misc_toks = []

    def MD(out, in_, w, slow=False):
        if slow:
            misc_toks.append(P.dma("pool", g_misc, lambda h: h.dma_start(out=out, in_=in_, allow_slow_non_contiguous=True), writes=[w]))
        else:
            misc_toks.append(P.dma("pool", g_misc, lambda h: h.dma_start(out=out, in_=in_), writes=[w]))

    MD(gm[:], g_mix_d.rearrange("(k p) -> p k", p=128), res("gm"), slow=True)
    MD(gf[:], g_ffn_d.rearrange("(k p) -> p k", p=128), res("gf"), slow=True)
    MD(gq8[0:64, :], g_q_d[:, :], res("gq8"))
    MD(gq8[64:128, :], g_q_d[:, :], res("gq8"))
    MD(gk1[0:64, :], g_k_d[:, :], res("gk1"))
    MD(gk1[64:128, :], g_k_d[:, :], res("gk1"))
    MD(ggla[:], g_gla_d[:, :], res("ggla"))
    sk = sinks_d.rearrange("o (i p) -> o p i", p=2)
    MD(esink[0:64, :], sk[:, 0, :].partition_broadcast(64), res("esink"), slow=True)
    MD(esink[64:128, :], sk[:, 1, :].partition_broadcast(64), res("esink"), slow=True)
    MD(hb[:], hb_d[:, :], res("hb"))
    MD(wal32[0:16, :], w_alpha_d[:, :], res("wal32"))
    MD(wal32[16:17, :], b_alpha_d[:, :], res("wal32"))
    for kvh in range(2):
        MD(ckd[:, kvh, 0:64], ck_d[:, kvh * 64:(kvh + 1) * 64], res("ckd"))
        MD(ckd[:, kvh, 64:128], ck_d[:, kvh * 64:(kvh + 1) * 64], res("ckd"))
    MD(cv32[:], cv_d[:, :], res("cv32"))

    misc_final = ("d", g_misc, g_misc.count)
    for nm in ("gm", "gf", "gq8", "gk1", "ggla", "esink", "hb", "wal32", "ckd", "cv32"):
        res(nm).w = misc_final
    OP("act", lambda h: h.mul(out=gq8[:], in_=gq8[:], mul=0.125), [res("gq8")], [res("gq8")])
    OP("act", lambda h: h.activation(out=esink[:], in_=esink[:], func=AF.Exp), [res("esink")], [res("esink")])
    OP("act", lambda h: h.copy(out=wal[:], in_=wal32[:]), [res("wal32")], [R_const])
    OP("pool", lambda h: h.memset(identf[:], 0.0), [], [res("identf")])
    OP("pool", lambda h: h.affine_select(out=identf[:], in_=identf[:], pattern=[[-1, 128]], compare_op=ALU.not_equal,
                                          fill=1.0, base=0, channel_multiplier=1), [res("identf")], [res("identf")])
    OP("pool", lambda h: h.tensor_copy(out=identb[:], in_=identf[:]), [res("identf")], [R_const])
    OP("pool", lambda h: h.memset(triI[:], -1.0 / 16), [], [res("triI")])
    OP("pool", lambda h: h.affine_select(out=triI[:], in_=triI[:], pattern=[[1, 128]], compare_op=ALU.is_ge,
                                          fill=0.0, base=0, channel_multiplier=-1), [res("triI")], [res("triI")])
    OP("pool", lambda h: h.memset(triR[:], -1.0 / 16), [], [res("triR")])
    OP("pool", lambda h: h.affine_select(out=triR[:], in_=triR[:], pattern=[[-1, 128]], compare_op=ALU.is_gt,
                                          fill=0.0, base=0, channel_multiplier=1), [res("triR")], [res("triR")])
    OP("pool", lambda h: h.memset(caus[:], 1.0), [], [res("caus")])
    OP("pool", lambda h: h.affine_select(out=caus[:], in_=caus[:], pattern=[[1, 128]], compare_op=ALU.is_ge,
                                          fill=0.0, base=0, channel_multiplier=-1), [res("caus")], [res("caus")])
    OP("pool", lambda h: h.memset(negcol[:], -1.0 / 16), [], [R_const])
    OP("pool", lambda h: h.memset(all1[:], 1.0), [], [R_const])
    OP("pool", lambda h: h.memset(blk1[:], 1.0), [], [res("blk1")])
    OP("pool", lambda h: h.memset(blk1[0:64, 64:128], 0.0), [res("blk1")], [res("blk1")])
    OP("pool", lambda h: h.memset(blk1[64:128, 0:64], 0.0), [res("blk1")], [res("blk1")])
    OP("pool", lambda h: h.memset(oneslo[:], 0.0), [], [res("oneslo")])
    OP("pool", lambda h: h.memset(oneslo[:, 0:64], 1.0), [res("oneslo")], [res("oneslo")])
    OP("pool", lambda h: h.memset(oneshi[:], 0.0), [], [res("oneshi")])
    OP("pool", lambda h: h.memset(oneshi[:, 64:128], 1.0), [res("oneshi")], [res("oneshi")])
    OP("pool", lambda h: h.memset(caT[:], 1.0), [], [res("caT")])
    OP("pool", lambda h: h.memset(Vlo[:], 0.0), [], [res("V")])
    OP("pool", lambda h: h.memset(Vhi[:], 0.0), [res("V")], [res("V")])
    OP("pool", lambda h: h.memset(S[:], 0.0), [], [res("S")])
    for p in range(2):
        OP("pool", lambda h, p=p: h.memset(kdecP[p][:], 0.0), [], [res("kdec")])
    for p in range(2):
        OP("pool", lambda h, p=p: h.memset(Sbz[p][:], 0.0), [], [res("Sb")])
        OP("pool", lambda h, p=p: h.memset(ktTz[p][:], 0.0), [], [res("ktT")])
        for kvh in range(2):
            OP("pool", lambda h, p=p, kvh=kvh: h.memset(KTz[kvh][p][:], 0.0), [], [res(f"KT{kvh}")])
    for i in range(1):
        OP("pool", lambda h, i=i: h.memset(PT[i][:], 0.0), [], [res(f"PT{i}")])
    for i in range(3):
        OP("pool", lambda h, i=i: h.memset(stg_out[i][:], 0.0), [], [res(f"stgo{i}")])
    CONSTS = [R_const, res("identf"), res("triI"), res("triR"), res("caus"), res("blk1"), res("oneslo"), res("oneshi"),
              res("gm"), res("gf"), res("gq8"), res("gk1"), res("ggla"), res("esink"), res("hb")]

    prep_jobs = {}
    prep_state = {"loaded": 0, "done": 0, "order": [], "prepped": set()}

    def prep_slab(sid, srcs, conv):
        prep_jobs[sid] = (srcs, conv)

    def prep_load(k):
        sid = prep_state["order"][k]
        i = k % 3
        srcs, _ = prep_jobs[sid]
        for (dfn, src) in srcs:
            P.dma("sp", g_si[i], lambda h, dfn=dfn, src=src, i=i: h.dma_start(out=dfn(stg_in[i]), in_=src), writes=[res(f"stgi{i}")])

    def prep_finish(k):
        sid = prep_state["order"][k]
        i = k % 3
        ri, ro = res(f"stgi{i}"), res(f"stgo{i}")
        _, (eng, fn) = prep_jobs[sid]
        OP(eng, lambda h, fn=fn, i=i: fn(h, stg_out[i], stg_in[i]), [ri] + CONSTS[8:10], [ro])
        P.dma("sp", g_so[i], lambda h, i=i, sid=sid: h.dma_start(out=wstream[sid], in_=stg_out[i][:]), reads=[ro], writes=[res(f"ws{sid}")])
        prep_state["prepped"].add(sid)

    def prep_step():
        k = prep_state["done"]
        n = len(prep_state["order"])
        if k >= n:
            return False
        while prep_state["loaded"] < min(n, k + 3):
            prep_load(prep_state["loaded"])
            prep_state["loaded"] += 1
        prep_finish(k)
        prep_state["done"] += 1
        return True

    def ensure_prepped(sid):
        while sid not in prep_state["prepped"]:
            assert prep_step()

    w_in_k = w_in_d.rearrange("(k p) n -> p k n", p=128)
    w_up_k = w_up_d.rearrange("(k p) n -> p k n", p=128)

    def v3(t, w=128):
        return t[:].rearrange("p (k n) -> p k n", k=8)[:, :, 0:w]

    def convA(gt, cs, w, engname):
        if cs != 1.0:
            engname = "dve"

        def fn(h, o, i):
            if engname == "dve":
                return h.scalar_tensor_tensor(out=v3(o, w), in0=v3(i, w), scalar=cs,
                                              in1=gt[:].unsqueeze(2).to_broadcast([128, 8, w]), op0=ALU.mult, op1=ALU.mult)
            return h.tensor_tensor(out=v3(o, w), in0=v3(i, w), in1=gt[:].unsqueeze(2).to_broadcast([128, 8, w]), op=ALU.mult)
        return (engname, fn)

    eng_rr = ["dve", "dve"]
    rr = [0]

    def next_eng():
        rr[0] += 1
        return eng_rr[rr[0] % 2]

    a_cols = [(0, 128), (128, 128), (256, 128), (384, 128),
              None, None,
              (768, 128), (896, 128),
              (1024, 128), (1152, 128),
              (1792, 128), (1920, 128), (2048, 128), (2176, 128),
              (2304, 16)]
    for ci, cw in enumerate(a_cols):
        if cw is None:
            kvh = ci - A_KA
            c0 = 512 + kvh * 64
            srcs = [(lambda t: t[:].rearrange("p (k n) -> p k n", k=8)[:, :, 0:64], w_in_k[:, :, c0:c0 + 64]),
                    (lambda t: t[:].rearrange("p (k n) -> p k n", k=8)[:, :, 64:128], w_in_k[:, :, c0:c0 + 64])]
            prep_slab(ci, srcs, convA(gm, 1.0, 128, next_eng()))
        else:
            c0, w = cw
            srcs = [(lambda t, w=w: v3(t, w), w_in_k[:, :, c0:c0 + w])]
            cs = 0.125 if ci in (A_QG, A_QG + 1) else 1.0
            prep_slab(ci, srcs, convA(gm, cs, w, next_eng()))
    for kc in range(8):
        rows = w_in_d[kc * 128:(kc + 1) * 128, :]
        srcs = [(lambda t: t[:, 0:512], rows[:, 1280:1792]),
                (lambda t: t[:, 512:640], rows[:, 640:768]),
                (lambda t: t[:, 640:896], rows[:, 1024:1280])]
        prep_slab(B0 + kc, srcs, ("act", lambda h, o, i, kc=kc: h.activation(out=o[:, 0:896], in_=i[:, 0:896], func=AF.Copy,
                                                                           scale=gm[:, kc:kc + 1])))
    for kc in range(8):
        srcs = [(lambda t: t[:], w_out_d[kc * 128:(kc + 1) * 128, :])]
        e = ["act", "dve"][kc % 2]
        if e == "act":
            prep_slab(O0 + kc, srcs, ("act", lambda h, o, i: h.copy(out=o[:], in_=i[:])))
        else:
            prep_slab(O0 + kc, srcs, (e, lambda h, o, i: h.tensor_copy(out=o[:], in_=i[:])))
    for f in range(32):
        srcs = [(lambda t: v3(t, 128), w_up_k[:, :, f * 128:(f + 1) * 128])]
        prep_slab(U0 + f, srcs, convA(gf, 1.0, 128, next_eng()))
    for f in range(32):
        srcs = [(lambda t: t[:], w_down_d[f * 128:(f + 1) * 128, :])]
        e = ["act", "dve"][f % 2]
        if e == "act":
            prep_slab(D0 + f, srcs, ("act", lambda h, o, i: h.copy(out=o[:], in_=i[:])))
        else:
            prep_slab(D0 + f, srcs, (e, lambda h, o, i: h.tensor_copy(out=o[:], in_=i[:])))
    prep_state["order"] = ([A_CA] + list(range(B0, B0 + 8)) + [A_KA, A_KA + 1] + [0, 10, 1, 11, 2, 12, 3, 13, 6, 8, 7, 9]
                           + list(range(O0, O0 + 8)))
    for G in range(4):
        prep_state["order"] += list(range(U0 + 8 * G, U0 + 8 * G + 8)) + list(range(D0 + 8 * G, D0 + 8 * G + 8))
    assert sorted(prep_state["order"]) == list(range(NSLAB))

    stream = []
    st_state = {"declared": 0, "released": 0, "flat": [], "gstart": []}

    def plan(groups):
        for g in groups:
            st_state["gstart"].append(len(st_state["flat"]))
            st_state["flat"].extend(g)
            stream.append(g)

    def pump():
        flat = st_state["flat"]
        while st_state["declared"] < len(flat) and st_state["declared"] < st_state["released"] + NS:
            n = st_state["declared"]
            slot = n % NS
            sid = flat[n]
            ensure_prepped(sid)
            P.dma("sp", g_ring[slot], lambda h, slot=slot, sid=sid: h.dma_start(out=ring[:, slot, :], in_=wstream[sid]),
                  reads=[res(f"ws{sid}")], writes=[R_ring[slot]])
            st_state["declared"] += 1

    gi = [0]

    def acquire():
        g = stream[gi[0]]
        start = st_state["gstart"][gi[0]]
        assert start == st_state["released"], (start, st_state["released"])
        assert start + len(g) <= st_state["released"] + NS
        pump()
        assert st_state["declared"] >= start + len(g)
        out = []
        for k in range(len(g)):
            slot = (start + k) % NS
            out.append((slot, R_ring[slot]))
        return out

    def release():
        g = stream[gi[0]]
        st_state["released"] += len(g)
        gi[0] += 1
        pump()

    A_MAIN = [0, 10, 1, 11, 2, 12, 3, 13, 4, 6, 5, 8, 7, 9, 14]

    def a_order(mode):
        return {"halo": [A_KA, A_KA + 1], "pre": [A_CA]}.get(mode, A_MAIN)

    def tile_groups(mode):
        if mode == "halo":
            return [[A_KA], [A_KA + 1], list(range(B0, B0 + 8))]
        if mode == "pre":
            return [[A_CA], list(range(B0, B0 + 8))]
        gs = [[i] for i in A_MAIN] + [list(range(B0, B0 + 8))] + [list(range(O0, O0 + 8))]
        for G in range(4):
            gs += [[U0 + 8 * G + j] for j in range(8)]
            gs += [list(range(D0 + 8 * G, D0 + 8 * G + 8))]
        return gs

    tiles = [("pre", i * MT) for i in range(npre)] + [("halo", npre * MT - 128)] + \
            [("main", npre * MT + i * MT) for i in range(SEG // MT)] + [("sample", 0)]
    if ntiles is not None:
        tiles = tiles[:ntiles]
    for (mode, _) in tiles:
        plan(tile_groups(mode))

    y_toks = []
    out_toks = []
    nbi = [0]
    pti = [0]
    acti = [0]

    def bf_view(bank):
        return banks[bank][:].bitcast(BF16).rearrange("p (k t) -> p k t", k=8)

    hook_holder = [None]

    def fire(name):
        if hook_holder[0] is not None:
            hook_holder[0](name)

    def run_tile(mode, tok0, buf, first_main=False, last_main=False):
        vout32 = vouts[1 if mode == "sample" else 0]
        kout32 = kouts[1 if mode == "sample" else 0]
        rvo = res("vout32_%d" % (1 if mode == "sample" else 0))
        rko = res("kout32_%d" % (1 if mode == "sample" else 0))
        xt = xts[buf]
        R_xt = R_xts[buf]
        sample = mode == "sample"
        TS = 32 if sample else 128
        nsub = 1 if (sample or mode == "halo") else 4
        NT = TS * nsub
        src = xs_d if sample else xin

        for s in range(nsub):
            P.dma("pool", g_x[s], lambda h, s=s: h.dma_start(out=xt[:TS, s, :], in_=src[tok0 + s * TS: tok0 + (s + 1) * TS, :]),
                  writes=[R_xt[s]])

        yield "load"

        def norm_A(col0, s):
            b = s
            rnb = res(f"nb{b}")
            c = col0 + s
            rq, rr2 = res(f"ssq{c}"), res(f"rstd{c}")
            OP("act", lambda h, s=s, b=b, c=c: h.activation(out=nb[b][:TS, :], in_=xt[:TS, s, :], func=AF.Square,
                                                          accum_out=ssq[:TS, c:c + 1]), [R_xt[s]], [rnb, rq])
            OP("act", lambda h, c=c: h.activation(out=rstd[:TS, c:c + 1], in_=ssq[:TS, c:c + 1], func=AF.Ln,
                                                  scale=1.0 / D, bias=EPS), [rq], [rr2])
            OP("act", lambda h, c=c: h.activation(out=rstd[:TS, c:c + 1], in_=rstd[:TS, c:c + 1], func=AF.Exp,
                                                  scale=-0.5), [rr2], [rr2])
            OP("dve", lambda h, s=s, b=b, c=c: h.tensor_scalar_mul(out=nb[b][:TS, :], in0=xt[:TS, s, :], scalar1=rstd[:TS, c:c + 1]),
               [R_xt[s], rr2], [rnb])

        def norm_B(s, nT, R_nT):
            b = s
            rnb = res(f"nb{b}")
            tp = bf_view(3)
            for kc in range(8):
                OP("pe", lambda h, kc=kc, b=b, tp=tp: h.transpose(out=tp[:, kc, :TS], in_=nb[b][:TS, kc * 128:(kc + 1) * 128],
                                                                identity=identb[:TS, :TS]), [rnb, R_const], [RB[3]])
            OP("dve", lambda h, s=s, tp=tp, nT=nT: h.tensor_copy(out=nT[:, :, s * TS:(s + 1) * TS], in_=tp[:, :, :TS]), [RB[3]], [R_nT[s]])

        def norm_T(col0, nT, R_nT):
            for s in range(nsub):
                norm_A(col0, s)
                norm_B(s, nT, R_nT)

        for s in range(nsub):
            norm_A(0, s)
            yield "normA"
        for s in range(nsub):
            norm_B(s, nTa, R_nTa)
        yield "front"
        nT, R_nT = nTa, R_nTa
        RnT = R_nT[:nsub]

        mmb = [0]

        def proj_chunk(M, evac):
            (slot, rr_), = acquire()
            bk = (0, 1, 6, 7)[mmb[0] % 4]
            mmb[0] += 1
            for kc in range(8):
                OP("pe", lambda h, kc=kc, slot=slot, bk=bk: h.matmul(banks[bk][:M, :NT], lhsT=ring[:, slot, kc * 128:kc * 128 + M],
                                                                   rhs=nTa[:, kc, :NT], start=(kc == 0), stop=(kc == 7)),
                   [rr_] + RnT, [RB[bk]])
            release()
            flush_pending()
            r = evac(bk)
            if callable(r):
                pending.append(r)

        pending = []

        def flush_pending():
            for f_ in pending:
                f_()
            del pending[:]

        def qknorm_evac(gvec, gres, outs):
            def ev(bk):
                OP("act", lambda h: h.activation(out=sqb[:, :NT], in_=banks[bk][:, :NT], func=AF.Square), [RB[bk]], [res("sqb")])
                return lambda: stage2(bk)

            def stage2(bk):
                OP("pe", lambda h: h.matmul(banks[2][:, :NT], lhsT=blk1[:], rhs=sqb[:, :NT], start=True, stop=True),
                   [res("sqb"), res("blk1")], [RB[2]])
                OP("act", lambda h: h.activation(out=rtmp[:, :NT], in_=banks[2][:, :NT], func=AF.Ln, scale=1.0 / 64, bias=EPS),
                   [RB[2]], [res("rtmp")])
                OP("act", lambda h: h.activation(out=rtmp[:, :NT], in_=rtmp[:, :NT], func=AF.Exp, scale=-0.5),
                   [res("rtmp")], [res("rtmp")])
                for (p0, p1, dfn, dres) in outs:
                    OP("dve", lambda h, dfn=dfn, p0=p0, p1=p1: h.scalar_tensor_tensor(out=dfn(), in0=banks[bk][p0:p1, :NT], scalar=gvec[p0:p1, 0:1],
                                                                       in1=rtmp[p0:p1, :NT], op0=ALU.mult, op1=ALU.mult),
                       [RB[bk], res("rtmp"), gres], [dres])
            return ev

        koff = 0 if mode == "halo" else 128
        want_kout = sample or last_main

        def chunk_qa(i):
            proj_chunk(128, qknorm_evac(gq8, res("gq8"), [(0, 128, lambda i=i: qaT[:, i, :NT], res("qaT"))]))

        def chunk_ka(kvh):
            outs = [(p * 64, (p + 1) * 64, lambda kvh=kvh, p=p: KTz[kvh][p][p * 64:(p + 1) * 64, koff:koff + NT], res(f"KT{kvh}")) for p in range(2)]
            if want_kout:
                def ev(bk, kvh=kvh, outs=outs):
                    base = qknorm_evac(gk1, res("gk1"), outs)
                    st2 = base(bk)
                    return lambda: (st2(), k32_extra(bk, kvh))

                def k32_extra(bk, kvh):
                    OP("dve", lambda h: h.scalar_tensor_tensor(out=k32T[kvh * 64:(kvh + 1) * 64, 0:TS],
                                                               in0=banks[bk][kvh * 64:(kvh + 1) * 64, NT - TS:NT], scalar=gk1[kvh * 64:(kvh + 1) * 64, 0:1],
                                                               in1=rtmp[kvh * 64:(kvh + 1) * 64, NT - TS:NT], op0=ALU.mult, op1=ALU.mult),
                       [RB[bk], res("rtmp"), res("gk1")], [res("k32T")])
                proj_chunk(128, ev)
            else:
                proj_chunk(128, qknorm_evac(gk1, res("gk1"), outs))

        def chunk_qg(i):
            proj_chunk(128, lambda bk, i=i: OP("act", lambda h: h.copy(out=qgT[:, i, :NT], in_=banks[bk][:, :NT]), [RB[bk]], [res("qgT")]))

        def chunk_kg(i):
            proj_chunk(128, lambda bk, i=i: OP("act", lambda h: h.copy(out=kgT[:, i, :NT], in_=banks[bk][:, :NT]), [RB[bk]], [res("kgT")]))

        def chunk_rg(i):
            def ev(bk, i=i):
                OP("act", lambda h: h.activation(out=sg[:, i, :NT], in_=banks[bk][:, :NT], func=AF.Exp, scale=-1.0), [RB[bk]], [res(f"sg{i}")])
                OP("dve", lambda h: h.tensor_scalar_add(out=sg[:, i, :NT], in0=sg[:, i, :NT], scalar1=1.0), [res(f"sg{i}")], [res(f"sg{i}")])
                OP("dve", lambda h: h.reciprocal(out=sg[:, i, :NT], in_=sg[:, i, :NT]), [res(f"sg{i}")], [res(f"sg{i}")])
                OP("dve", lambda h: h.tensor_tensor(out=sg[:, i, :NT], in0=banks[bk][:, :NT], in1=sg[:, i, :NT], op=ALU.mult),
                   [RB[bk], res(f"sg{i}")], [res(f"sg{i}")])
            proj_chunk(128, ev)

        def chunk_ca():
            proj_chunk(128, lambda bk: OP("act", lambda h: h.copy(out=caT[0:16, :NT], in_=banks[bk][0:16, :NT]), [RB[bk]], [res("caT")]))

        for cid in a_order(mode):
            if cid < 4:
                chunk_qa(cid)
            elif cid < 6:
                chunk_ka(cid - 4)
            elif cid < 8:
                chunk_qg(cid - 6)
            elif cid < 10:
                chunk_kg(cid - 8)
            elif cid < 14:
                chunk_rg(cid - 10)
            else:
                chunk_ca()

        flush_pending()
        slots = acquire()
        for s in range(nsub):
            for (bk, c0, cw) in ((4, 0, 512), (5, 640, 256) if mode == "pre" else (5, 512, 384)):
                for kc in range(8):
                    slot, rr_ = slots[kc]
                    OP("pe", lambda h, kc=kc, slot=slot, bk=bk, c0=c0, cw=cw, s=s: h.matmul(
                        banks[bk][:TS, :cw], lhsT=nTa[:, kc, s * TS:(s + 1) * TS], rhs=ring[:, slot, c0:c0 + cw],
                        start=(kc == 0), stop=(kc == 7)), [rr_, R_nT[s]], [RB[bk]])
            if mode != "halo":
                OP("act", lambda h, s=s: h.copy(out=vgTok[:TS, s, :], in_=banks[4][:TS, :]), [RB[4]], [res("vgTok")])
                kg0 = 0 if mode == "pre" else 128
                OP("act", lambda h, s=s, kg0=kg0: h.copy(out=kgTok[:TS, s, :], in_=banks[5][:TS, kg0:kg0 + 256]), [RB[5]], [res("kgTok")])
            if mode != "pre":
                vt = 0 if mode == "halo" else s + 1
                OP("act", lambda h, vt=vt: h.copy(out=Vlo[:TS, vt, :, 0:64], in_=banks[5][:TS, 0:128].rearrange("p (a d) -> p a d", a=2)),
                   [RB[5]], [res("V")])
                OP("act", lambda h, vt=vt: h.copy(out=Vhi[:TS, vt, :, 64:128], in_=banks[5][:TS, 0:128].rearrange("p (a d) -> p a d", a=2)),
                   [RB[5]], [res("V")])
                if want_kout and s == nsub - 1:
                    OP("act", lambda h: h.copy(out=vout32[:TS, :], in_=banks[5][:TS, 0:128]), [RB[5]], [rvo])
            if mode == "pre":
                fire("Bsub")
        release()
        if mode == "halo":
            yield "mid"
            return

        def gla_batch():
            W = 256 * nsub
            R67 = [RB[6], RB[7]]
            for s in range(nsub):
                OP("pe", lambda h, s=s: h.matmul(U4[:TS, s * 256:(s + 1) * 256], lhsT=caT[0:17, s * TS:(s + 1) * TS], rhs=wal[0:17, :],
                                                 start=True, stop=True), [res("caT"), R_const], R67)
            OP("act", lambda h: h.activation(out=eu[:TS, :W], in_=U4[:TS, :W], func=AF.Exp, scale=-1.0), R67, [res("eu")])
            if mode == "pre":
                fire("after_rev")
            OP("act", lambda h: h.activation(out=eu[:TS, :W], in_=eu[:TS, :W], func=AF.Ln, bias=1.0), [res("eu")], [res("eu")])
            euv = eu[:TS, :W].rearrange("p (a n) -> p a n", a=nsub)
            OP("act", lambda h: h.copy(out=Lh[:TS, 0:nsub, :], in_=euv), [res("eu")], [res("Lt")])
            OP("dve", lambda h: h.tensor_tensor(out=Ll[:TS, 0:nsub, :], in0=euv, in1=Lh[:TS, 0:nsub, :], op=ALU.subtract),
               [res("eu"), res("Lt")], [res("Lt")])
            for s in range(nsub):
                for k_, Lx in enumerate((Lh, Ll)):
                    OP("pe", lambda h, s=s, k_=k_, Lx=Lx: h.matmul(U4[:TS, s * 256:(s + 1) * 256], lhsT=triR[:TS, :TS], rhs=Lx[:TS, s, :],
                                                                 start=(k_ == 0), stop=(k_ == 1)), [res("Lt"), res("triR")], R67)
                for i in range(2):
                    for k_, Lx in enumerate((Lh, Ll)):
                        OP("pe", lambda h, s=s, i=i, k_=k_, Lx=Lx: h.matmul(banks[2][:, 2 * s + i:2 * s + i + 1], lhsT=Lx[:TS, s, i * 128:(i + 1) * 128],
                                                                          rhs=negcol[:TS, 0:1], start=(k_ == 0), stop=(k_ == 1)),
                           [res("Lt"), R_const], [RB[2]])
            OP("act", lambda h: h.activation(out=erev[:TS, :W], in_=U4[:TS, :W], func=AF.Exp), R67, [res("eu")])
            OP("act", lambda h: h.activation(out=dec[:, 0:2 * nsub], in_=banks[2][:, 0:2 * nsub], func=AF.Exp), [RB[2]], [res("dec")])
            for p in range(2):
                v5 = lambda t, p=p: t.rearrange("q a (i r d) -> q a i r d", i=2, r=2)[:TS, 0:nsub, :, p, :]
                OP("dve", lambda h, p=p, v5=v5: h.tensor_tensor(out=v5(kdecP[p][:]), in0=v5(kgTok[:]),
                                                               in1=v5(erev[:, 0:1024].rearrange("q (a n) -> q a n", a=4)), op=ALU.mult),
                   [res("kgTok"), res("eu")], [res("kdec")])

        def gla_state(s):
            sbk = (6 + s % 2) if mode == "pre" else 3
            for i in range(2):
                for p in range(2):
                    OP("pe", lambda h, i=i, p=p: h.matmul(banks[sbk][:, i * 128:(i + 1) * 128], lhsT=kdecP[p][:TS, s, i * 128:(i + 1) * 128],
                                                         rhs=vgTok[:TS, s, (2 * i + p) * 128:(2 * i + p + 1) * 128], start=(p == 0), stop=(p == 1)),
                       [res("kdec"), res("vgTok")], [RB[sbk]])
            for i in range(2):
                OP("dve", lambda h, i=i: h.scalar_tensor_tensor(
                    out=S[:, i, :], in0=S[:, i, :], scalar=dec[:, 2 * s + i:2 * s + i + 1],
                    in1=banks[sbk][:, i * 128:(i + 1) * 128], op0=ALU.mult, op1=ALU.add),
                   [res("S"), res("dec"), RB[sbk]], [res("S")])

        if mode == "pre":
            gla_batch()
            for s in range(nsub):
                gla_state(s)
            yield "mid"
            return

        if sample:
            for kvh in range(2):
                OP("pe", lambda h, kvh=kvh: h.transpose(out=banks[7][:, kvh * 128:(kvh + 1) * 128], in_=ckd[:, kvh, :], identity=identf[:]),
                   [res("ckd"), res("identf")], [RB[7]])
                for p in range(2):
                    OP("act", lambda h, kvh=kvh, p=p: h.copy(out=KTz[kvh][p][p * 64:(p + 1) * 64, 0:128],
                                                             in_=banks[7][p * 64:(p + 1) * 64, kvh * 128:(kvh + 1) * 128]), [RB[7]], [res(f"KT{kvh}")])
            OP("dve", lambda h: h.tensor_copy(out=Vlo[:, 0, :, 0:64], in_=cv32[:].rearrange("p (a d) -> p a d", a=2)), [res("cv32")], [res("V")])
            OP("dve", lambda h: h.tensor_copy(out=Vhi[:, 0, :, 64:128], in_=cv32[:].rearrange("p (a d) -> p a d", a=2)), [res("cv32")], [res("V")])
            for hh in range(4):
                P.dma("pool", g_misc, lambda h, hh=hh: h.dma_start(out=S[(hh % 2) * 64:(hh % 2 + 1) * 64, hh // 2, :], in_=st_d[hh]),
                      writes=[res("S")])
            for p in range(2):
                OP("act", lambda h, p=p: h.copy(out=Sbz[p][p * 64:(p + 1) * 64, :, :], in_=S[p * 64:(p + 1) * 64, :, :]), [res("S")], [res("Sb")])

        if first_main:
            for p in range(2):
                OP("act", lambda h, p=p: h.copy(out=Sbz[p][p * 64:(p + 1) * 64, :, :], in_=S[p * 64:(p + 1) * 64, :, :]), [res("S")], [res("Sb")])

        def attention(s):
            pb = 0
            pti[0] += 1
            rPT = res(f"PT{pb}")
            sbank = {(0, 0): 0, (0, 1): 1, (1, 0): 4, (1, 1): 5}
            nk = [128, TS]
            for kvh in range(2):
                for kt in range(2):
                    bk = sbank[(kt, kvh)]
                    kcol = s * 128 if kt == 0 else 128 + s * TS
                    for par in range(2):
                        for cl in range(2):
                            OP("pe", lambda h, kvh=kvh, kt=kt, bk=bk, kcol=kcol, par=par, cl=cl: h.matmul(
                                banks[bk][:nk[kt], (par * 2 + cl) * TS:(par * 2 + cl + 1) * TS],
                                lhsT=KTz[kvh][par][:, kcol:kcol + nk[kt]],
                                rhs=qaT[:, 2 * kvh + cl, s * TS:(s + 1) * TS],
                                start=True, stop=True), [res(f"KT{kvh}"), res("qaT")], [RB[bk]])
                    src4 = banks[bk][:, 0:4 * TS].rearrange("p (a q) -> p a q", a=4)
                    dst4 = PT[pb][:, kt, kvh, 0:4 * TS].rearrange("p (a q) -> p a q", a=4)
                    if sample:
                        regs = [(0, nk[kt], 0, TS)]
                    elif kt == 0:
                        regs = [(0, 64, 0, 64), (64, 128, 0, 128)]
                    else:
                        regs = [(0, 64, 0, 128), (64, 128, 64, 128)]
                    for (p0, p1, q0, q1) in regs:
                        if first_main and s == 0 and kt == 0:
                            OP("act", lambda h, p0=p0, p1=p1, q0=q0, q1=q1, src4=src4, dst4=dst4: h.activation(
                                out=dst4[p0:p1, :, q0:q1], in_=src4[p0:p1, :, q0:q1], func=AF.Exp, bias=hb[p0:p1, 0:1]),
                               [RB[bk], res("hb")], [rPT])
                        else:
                            OP("act", lambda h, p0=p0, p1=p1, q0=q0, q1=q1, src4=src4, dst4=dst4: h.activation(
                                out=dst4[p0:p1, :, q0:q1], in_=src4[p0:p1, :, q0:q1], func=AF.Exp), [RB[bk]], [rPT])
        def att_B(s):
            pb = 0
            rPT = res(f"PT{pb}")
            nk = [128, TS]
            for (bk, lo, hi, rl) in ((6, None, None, "V"), (7, oneslo, oneshi, None)):
                for i in range(4):
                    kvh = i // 2
                    cl = i % 2
                    n = 0
                    for kt in range(2):
                        vt = s if kt == 0 else s + 1
                        for par in range(2):
                            if lo is None:
                                lhs = (Vlo if par == 0 else Vhi)[:nk[kt], vt, kvh, :]
                                rd = [res("V"), rPT]
                            else:
                                lhs = (lo if par == 0 else hi)[:nk[kt], :]
                                rd = [res("oneslo"), res("oneshi"), rPT]
                            col = (par * 2 + cl) * TS
                            OP("pe", lambda h, bk=bk, i=i, lhs=lhs, kt=kt, kvh=kvh, col=col, n=n: h.matmul(
                                banks[bk][:, i * TS:(i + 1) * TS], lhsT=lhs, rhs=PT[pb][:nk[kt], kt, kvh, col:col + TS],
                                start=(n == 0), stop=(n == 3)), rd, [RB[bk]])
                            n += 1
        def att_C(s):
            W4 = 4 * TS
            OP("dve", lambda h: h.tensor_tensor(out=dn[:, 0:W4].rearrange("p (a q) -> p a q", a=4),
                                                in0=banks[7][:, 0:W4].rearrange("p (a q) -> p a q", a=4),
                                                in1=esink[:].unsqueeze(2).to_broadcast([128, 4, TS]), op=ALU.add),
               [RB[7], res("esink")], [res("dn")])
            OP("dve", lambda h: h.reciprocal(out=dn[:, 0:W4], in_=dn[:, 0:W4]), [res("dn")], [res("dn")])
            OP("dve", lambda h: h.tensor_tensor(out=mixT[:, 0:4, s * TS:(s + 1) * TS], in0=banks[6][:, 0:W4].rearrange("p (a q) -> p a q", a=4),
                                                in1=dn[:, 0:W4].rearrange("p (a q) -> p a q", a=4), op=ALU.mult),
               [RB[6], res("dn")], [res(f"mixA{s}")])

        def g1(s):
            for i in range(2):
                for k_, Lx in enumerate((Lh, Ll)):
                    OP("pe", lambda h, i=i, k_=k_, Lx=Lx: h.matmul(banks[2][:, i * TS:(i + 1) * TS], lhsT=Lx[:TS, s, i * 128:(i + 1) * 128],
                                                                 rhs=triI[:TS, :TS], start=(k_ == 0), stop=(k_ == 1)), [res("Lt"), res("triI")], [RB[2]])
            W2 = 2 * TS
            OP("act", lambda h: h.activation(out=E1[:, 0:W2], in_=banks[2][:, 0:W2], func=AF.Exp), [RB[2]], [res("E1")])
            OP("act", lambda h: h.activation(out=E2[:, 0:W2], in_=banks[2][:, 0:W2], func=AF.Exp, scale=-1.0), [RB[2]], [res("E2")])
            OP("dve", lambda h: h.tensor_tensor(out=qtT[:, :, :TS], in0=qgT[:, :, s * TS:(s + 1) * TS],
                                                in1=E1[:, 0:W2].rearrange("p (a t) -> p a t", a=2), op=ALU.mult),
               [res("qgT"), res("E1")], [res("qtT")])
            for p in range(2):
                OP("dve", lambda h, p=p: h.tensor_tensor(out=ktTz[p][p * 64:(p + 1) * 64, :, :TS], in0=kgT[p * 64:(p + 1) * 64, :, s * TS:(s + 1) * TS],
                                                         in1=E2[p * 64:(p + 1) * 64, 0:W2].rearrange("p (a t) -> p a t", a=2), op=ALU.mult),
                   [res("kgT"), res("E2")], [res("ktT")])
        def g2(s):
            for hh in range(4):
                p, i = hh % 2, hh // 2
                OP("pe", lambda h, hh=hh, p=p, i=i: h.matmul(banks[3][:TS, hh * TS:(hh + 1) * TS], lhsT=ktTz[p][:, i, :TS],
                                                           rhs=qtT[:, i, :TS], start=True, stop=True),
                   [res("ktT"), res("qtT")], [RB[3]])
            W4 = 4 * TS
            OP("dve", lambda h: h.tensor_tensor(out=AT[:TS, :, :TS], in0=banks[3][:TS, 0:W4].rearrange("p (a t) -> p a t", a=4),
                                                in1=caus[:TS, :TS].unsqueeze(1).to_broadcast([TS, 4, TS]), op=ALU.mult),
               [RB[3], res("caus")], [res("AT")])

        def g3(s):
            for hh in range(4):
                p, i = hh % 2, hh // 2
                OP("pe", lambda h, hh=hh: h.matmul(banks[2][:, hh * TS:(hh + 1) * TS], lhsT=vgTok[:TS, s, hh * 128:(hh + 1) * 128],
                                                   rhs=AT[:TS, hh, :TS], start=True, stop=False), [res("vgTok"), res("AT")], [RB[2]])
                OP("pe", lambda h, hh=hh, p=p, i=i: h.matmul(banks[2][:, hh * TS:(hh + 1) * TS], lhsT=Sbz[p][:, i, :],
                                                           rhs=qtT[:, i, :TS], start=False, stop=True),
                   [res("Sb"), res("qtT")], [RB[2]])

        def g4(s):
            gla_state(s)
            for p in range(2):
                OP("act", lambda h, p=p: h.copy(out=Sbz[p][p * 64:(p + 1) * 64, :, :], in_=S[p * 64:(p + 1) * 64, :, :]), [res("S")], [res("Sb")])

        def g5(s):
            W4 = 4 * TS
            OP("act", lambda h: h.activation(out=sqb[:, :W4], in_=banks[2][:, :W4], func=AF.Square), [RB[2]], [res("sqb")])
            OP("pe", lambda h: h.matmul(banks[3][:, :W4], lhsT=all1[:], rhs=sqb[:, :W4], start=True, stop=True), [res("sqb"), R_const], [RB[3]])
            OP("act", lambda h: h.activation(out=rtmp[:, :W4], in_=banks[3][:, :W4], func=AF.Ln, scale=1.0 / 128, bias=EPS), [RB[3]], [res("rtmp")])
            OP("act", lambda h: h.activation(out=rtmp[:, :W4], in_=rtmp[:, :W4], func=AF.Exp, scale=-0.5), [res("rtmp")], [res("rtmp")])
            OP("dve", lambda h: h.scalar_tensor_tensor(out=otmp[:, :W4], in0=banks[2][:, :W4], scalar=ggla[:, 0:1], in1=rtmp[:, :W4],
                                                       op0=ALU.mult, op1=ALU.mult), [RB[2], res("rtmp"), res("ggla")], [res("dn")])
            OP("dve", lambda h: h.tensor_tensor(out=mixT[:, 4:8, s * TS:(s + 1) * TS], in0=otmp[:, :W4].rearrange("p (a t) -> p a t", a=4),
                                                in1=sg[:, :, s * TS:(s + 1) * TS], op=ALU.mult), [res("dn")] + [res(f"sg{i_}") for i_ in range(4)], [res(f"mixG{s}")])

        gla_batch()
        for s in range(nsub):
            g1(s)
            attention(s)
            g2(s)
            g3(s)
            att_B(s)
            g4(s)
            att_C(s)
            g5(s)

        if want_kout:
            kd, vd, sd = (ksam_d, vsam_d, ssam_d) if sample else (kp_d, vp_d, sp_d)
            OP("pe", lambda h: h.transpose(out=banks[7][:TS, 0:128], in_=k32T[:, :TS], identity=identf[:]), [res("k32T"), res("identf")], [RB[7]])
            OP("act", lambda h: h.copy(out=kout32[:TS, :], in_=banks[7][:TS, 0:128]), [RB[7]], [rko])
            out_toks.append(P.dma("pool", g_out, lambda h: h.dma_start(out=kd[:, :], in_=kout32[:TS, :]), reads=[rko]))
            out_toks.append(P.dma("pool", g_out, lambda h: h.dma_start(out=vd[:, :], in_=vout32[:TS, :]), reads=[rvo]))
            for hh in range(4):
                out_toks.append(P.dma("pool", g_out, lambda h, hh=hh: h.dma_start(out=sd[hh], in_=S[(hh % 2) * 64:(hh % 2 + 1) * 64, hh // 2, :]),
                                      reads=[res("S")]))
        if mode == "main":
            for kvh in range(2):
                for p in range(2):
                    OP("dve", lambda h, kvh=kvh, p=p: h.tensor_copy(out=KTz[kvh][p][:, 0:128], in_=KTz[kvh][p][:, MT:MT + 128]),
                       [res(f"KT{kvh}")], [res(f"KT{kvh}")])
            OP("dve", lambda h: h.tensor_copy(out=Vlo[:, 0, :, :], in_=Vlo[:, 4, :, :]), [res("V")], [res("V")])
            OP("dve", lambda h: h.tensor_copy(out=Vhi[:, 0, :, :], in_=Vhi[:, 4, :, :]), [res("V")], [res("V")])

        slots = acquire()
        for s in range(nsub):
            for half in range(2):
                bk = 4 + half
                for kc in range(8):
                    slot, rr_ = slots[kc]
                    OP("pe", lambda h, kc=kc, slot=slot, bk=bk, half=half, s=s: h.matmul(
                        banks[bk][:TS, :], lhsT=mixT[:, kc, s * TS:(s + 1) * TS], rhs=ring[:, slot, half * 512:(half + 1) * 512],
                        start=(kc == 0), stop=(kc == 7)), [rr_, res(f"mixA{s}"), res(f"mixG{s}")], [RB[bk]])
                OP("dve", lambda h, s=s, half=half, bk=bk: h.tensor_tensor(out=xt[:TS, s, half * 512:(half + 1) * 512],
                                                                          in0=banks[bk][:TS, :], in1=xt[:TS, s, half * 512:(half + 1) * 512],
                                                                          op=ALU.add), [RB[bk], R_xt[s]], [R_xt[s]])
        release()

        norm_T(4, nTb, R_nTb)
        nT, R_nT = nTb, R_nTb
        RnT = R_nT[:nsub]

        for G in range(4):
            if G == 2:
                yield "mid"
            ab = acti[0] % 2
            acti[0] += 1
            rA = res(f"actT{ab}")
            for j in range(8):
                (slot, rr_), = acquire()
                bk = (0, 1, 6, 7)[mmb[0] % 4]
                rb = mmb[0] % 2
                mmb[0] += 1
                for kc in range(8):
                    OP("pe", lambda h, kc=kc, slot=slot, bk=bk: h.matmul(banks[bk][:, :NT], lhsT=ring[:, slot, kc * 128:(kc + 1) * 128],
                                                                       rhs=nTb[:, kc, :NT], start=(kc == 0), stop=(kc == 7)),
                       [rr_] + RnT, [RB[bk]])
                release()
                OP("act", lambda h, bk=bk, rb=rb: h.activation(out=relu_t[rb][:, :NT], in_=banks[bk][:, :NT], func=AF.Relu),
                   [RB[bk]], [res(f"relu{rb}")])
                OP("act", lambda h, rb=rb, ab=ab, j=j: h.activation(out=actT[ab][:, j, :NT], in_=relu_t[rb][:, :NT], func=AF.Square),
                   [res(f"relu{rb}")], [rA])
            slots = acquire()
            for s in range(nsub):
                for half in range(2):
                    bk = 4 + half
                    for j in range(8):
                        slot, rr_ = slots[j]
                        OP("pe", lambda h, j=j, slot=slot, bk=bk, half=half, s=s, ab=ab: h.matmul(
                            banks[bk][:TS, :], lhsT=actT[ab][:, j, s * TS:(s + 1) * TS], rhs=ring[:, slot, half * 512:(half + 1) * 512],
                            start=(j == 0), stop=(j == 7)), [rr_, rA], [RB[bk]])
                    OP("dve", lambda h, s=s, half=half, bk=bk: h.tensor_tensor(out=xt[:TS, s, half * 512:(half + 1) * 512],
                                                                              in0=banks[bk][:TS, :], in1=xt[:TS, s, half * 512:(half + 1) * 512],
                                                                              op=ALU.add), [RB[bk], R_xt[s]], [R_xt[s]])
            release()

        dst = ys_d if sample else y_d
        o0 = 0 if sample else tok0 - npre * MT
        for s in range(nsub):
            y_toks.append(P.dma("pool", g_y[s], lambda h, s=s: h.dma_start(out=dst[o0 + s * TS:o0 + (s + 1) * TS, :], in_=xt[:TS, s, :]),
                                reads=[R_xt[s]], writes=[]))

    gens = []
    mi = 0
    for ti, (mode, tok0) in enumerate(tiles):
        if mode == "main":
            gens.append(run_tile(mode, tok0, ti % 2, first_main=(mi == 0), last_main=(mi == nmain - 1)))
            mi += 1
        else:
            gens.append(run_tile(mode, tok0, ti % 2))

    n_t = len(gens)
    pos = ["start"] * n_t
    nA_left = [(1 if tiles[i][0] in ("halo", "sample") else 4) for i in range(n_t)]

    def step1(i):
        pos[i] = next(gens[i], "end")
        if pos[i] == "normA":
            nA_left[i] -= 1
        return pos[i]

    def step_to(i, label):
        order = ["start", "load", "normA", "front", "mid", "end"]
        while order.index(pos[i]) < order.index(label) or (label == "normA" and pos[i] != "normA"):
            step1(i)
            if pos[i] == label:
                break
        assert pos[i] == label or (label == "front" and pos[i] == "front"), (i, pos[i], label)

    cur = [0]

    def hook(name):
        nx = cur[0] + 1
        if nx >= n_t:
            return
        if name == "Bsub":
            if pos[nx] in ("load", "normA") and nA_left[nx] > 0:
                step1(nx)
        elif name == "after_rev":
            while pos[nx] != "front":
                step1(nx)

    hook_holder[0] = hook
    for _ in range(11):
        prep_step()
    if n_t:
        step_to(0, "load")
        while pos[0] != "front":
            step1(0)
    if n_t > 1:
        step_to(1, "load")
    for ti in range(n_t):
        cur[0] = ti
        while pos[ti] != "mid":
            step1(ti)
        for _ in range(6):
            prep_step()
        if ti + 1 < n_t:
            while pos[ti + 1] != "front":
                step1(ti + 1)
        while pos[ti] != "end":
            step1(ti)
        if ti + 2 < n_t:
            while pos[ti + 2] != "load":
                step1(ti + 2)

    P.wait_all("pool", y_toks[-4:] + out_toks[-1:] + misc_toks[-1:])
    P.wait_all("sp", [("d", g, g.count) for g in g_ring if g.count > 0])
    assert gi[0] == len(stream)
    P.emit(block, sems)
    es.close()
    return nc


_CACHE = {}


def kernel(x_prompt, x_sample, cache_k, cache_v, state_gla, g_mix, w_in, w_alpha, b_alpha, g_q, g_k, sinks,
           g_gla_out, w_out, g_ffn, w_up, w_down):
    f = lambda a: np.ascontiguousarray(np.asarray(a, dtype=np.float32))
    x_prompt, x_sample = f(x_prompt), f(x_sample)
    B, T, _ = x_prompt.shape
    npre = NPRE
    if "nc" not in _CACHE:
        _CACHE["nc"] = build_program(npre)
    nc = _CACHE["nc"]
    common = {
        "g_mix": f(g_mix).reshape(D), "w_in": f(w_in).reshape(D, 2320), "w_alpha": f(w_alpha).reshape(16, 256),
        "b_alpha": f(b_alpha).reshape(1, 256), "g_q": f(g_q).reshape(64, 1), "g_k": f(g_k).reshape(64, 1),
        "sinks": f(sinks).reshape(1, 8), "g_gla": f(g_gla_out).reshape(128, 1), "w_out": f(w_out).reshape(D, D),
        "g_ffn": f(g_ffn).reshape(D), "w_up": f(w_up).reshape(D, 4096), "w_down": f(w_down).reshape(4096, D),
    }
    in_maps = []
    PRE = npre * MT
    for c in range(8):
        b, j = c // 4, c % 4
        start = j * SEG
        xin = np.zeros((PRE + SEG, D), np.float32)
        lo = max(0, start - PRE)
        xin[PRE - (start - lo):PRE + SEG] = x_prompt[b, lo:start + SEG]
        hbv = np.full((128, 1), 0.0 if j > 0 else -30000.0, np.float32)
        m = dict(common)
        m.update({"xin": xin, "xs": x_sample[c], "ck": f(cache_k)[0, c].reshape(128, 128), "cv": f(cache_v)[0, c].reshape(128, 128),
                  "st": f(state_gla)[0, c], "hb": hbv})
        in_maps.append(m)
    res = run_bass_kernel_spmd(nc, in_maps, core_ids=list(range(8)))
    r = res.results
    y_prompt = np.stack([np.concatenate([r[b * 4 + j]["y"] for j in range(4)], axis=0) for b in range(2)])
    y_sample = np.stack([r[c]["ys"] for c in range(8)])
    k_prompt = np.stack([r[b * 4 + 3]["kp"].reshape(128, 2, 64) for b in range(2)])[None]
    v_prompt = np.stack([r[b * 4 + 3]["vp"].reshape(128, 2, 64) for b in range(2)])[None]
    s_prompt = np.stack([r[b * 4 + 3]["sp"] for b in range(2)])[None]
    k_sample = np.stack([r[c]["ksam"].reshape(32, 2, 64) for c in range(8)])[None]
    v_sample = np.stack([r[c]["vsam"].reshape(32, 2, 64) for c in range(8)])[None]
    s_sample = np.stack([r[c]["ssam"] for c in range(8)])[None]
    return (y_prompt.astype(np.float32), y_sample.astype(np.float32), k_prompt.astype(np.float32), v_prompt.astype(np.float32),
            s_prompt.astype(np.float32), k_sample.astype(np.float32), v_sample.astype(np.float32), s_sample.astype(np.float32))
```

```python
import os
import numpy as np
from contextlib import ExitStack
import concourse.bass as bass
import concourse.mybir as mybir
from concourse.bass_utils import run_bass_kernel_spmd

F32 = mybir.dt.float32
BF16 = mybir.dt.bfloat16
AF = mybir.ActivationFunctionType
ALU = mybir.AluOpType

D = 1024
SEG = 4096
MT = 512
NPRE = 24
NS = 14
EPS = 1e-6
SAME_ENGINE_SYNC = os.environ.get("NOSES", "") == ""

A_QA, A_KA, A_QG, A_KG, A_RG, A_CA = 0, 4, 6, 8, 10, 14
B0, O0, U0, D0 = 15, 23, 31, 63
NSLAB = 95
SERIAL = os.environ.get("SERIAL", "") != ""


class StopTile(Exception):
    pass


class Res:
    __slots__ = ("name", "w", "r")

    def __init__(self, name):
        self.name = name
        self.w = None
        self.r = []


class Eng:
    def __init__(self, name):
        self.name = name
        self.is_pe = name == "pe"
        self.items = []
        self.n = 0
        self.needed = set()
        self.seen = {}


class DmaGroup:
    def __init__(self, name):
        self.name = name
        self.count = 0
        self.sem = None


class Prog:
    def __init__(self):
        self.engs = {k: Eng(k) for k in ("pe", "act", "dve", "pool", "sp")}
        self.groups = []

    def group(self, name):
        g = DmaGroup(name)
        self.groups.append(g)
        return g

    def _deps(self, eng, reads, writes):
        deps = []
        for r in reads:
            if r.w is not None:
                deps.append((r.w, True))
        for w in writes:
            if w.w is not None:
                deps.append((w.w, False))
            deps.extend((t_, False) for t_ in w.r)
        out = {}
        for t, raw in deps:
            kind, src, idx = t
            if kind == "e" and src is eng and (eng.is_pe or not SAME_ENGINE_SYNC or not raw):
                continue
            key = (kind, id(src))
            if eng.seen.get(key, 0) >= idx:
                continue
            if key not in out or out[key][2] < idx:
                out[key] = t
        for key, t in out.items():
            eng.seen[key] = t[2]
            if t[0] == "e":
                t[1].needed.add(t[2])
        return list(out.values())

    def _serial(self, eng, waits):
        lt = getattr(self, "last_tok", None)
        if SERIAL and lt is not None and not (lt[0] == "e" and lt[1] is eng and eng.is_pe):
            key = (lt[0], id(lt[1]))
            if eng.seen.get(key, 0) < lt[2]:
                eng.seen[key] = lt[2]
                if lt[0] == "e":
                    lt[1].needed.add(lt[2])
                waits = [w for w in waits if (w[0], id(w[1])) != key] + [lt]
        return waits

    def _finish(self, tok, reads, writes):
        self.last_tok = tok
        for r in reads:
            if len(r.r) > 64:
                r.r = r.r[-64:] if r.w is None else r.r
            r.r.append(tok)
        for w in writes:
            w.w = tok
            w.r = []
        return tok

    def op(self, engname, fn, reads=(), writes=()):
        eng = self.engs[engname]
        waits = self._serial(eng, self._deps(eng, reads, writes))
        eng.n += 1
        tok = ("e", eng, eng.n)
        eng.items.append((waits, fn, eng.n, None))
        return self._finish(tok, reads, writes)

    def dma(self, engname, group, fn, reads=(), writes=()):
        eng = self.engs[engname]
        waits = self._serial(eng, self._deps(eng, reads, writes))
        if group.count > 0 and not getattr(group, "multi", False):
            key = ("d", id(group))
            if eng.seen.get(key, 0) < group.count:
                eng.seen[key] = group.count
                waits = [w for w in waits if (w[0], id(w[1])) != key] + [("d", group, group.count)]
        eng.n += 1
        group.count += 1
        tok = ("d", group, group.count)
        eng.items.append((waits, fn, eng.n, group))
        return self._finish(tok, reads, writes)

    def wait_all(self, engname, toks):
        eng = self.engs[engname]
        waits = []
        for t in toks:
            if t is None:
                continue
            if t[0] == "e":
                t[1].needed.add(t[2])
            waits.append(t)
        eng.items.append((waits, None, None, None))

    def emit(self, block, sems):
        handles = {"pe": "tensor", "act": "scalar", "dve": "vector", "pool": "gpsimd", "sp": "sync"}
        for e in self.engs.values():
            e.sigcount = {}
            c = 0
            for (_, fn, idx, grp) in e.items:
                if idx is not None and grp is None and idx in e.needed:
                    c += 1
                    e.sigcount[idx] = c
        prog = self

        def make(ename):
            e = prog.engs[ename]

            def body(h):
                for (waits, fn, idx, grp) in e.items:
                    for t in waits:
                        if t[0] == "e":
                            h.wait_ge(sems[t[1].name], t[1].sigcount[t[2]])
                        else:
                            h.wait_ge(t[1].sem, 16 * t[2])
                    if fn is None:
                        continue
                    ins = fn(h)
                    if grp is not None:
                        ins.then_inc(grp.sem, 16)
                    elif idx in e.needed:
                        ins.then_inc(sems[ename], 1)
            return body

        for ename, attr in handles.items():
            if prog.engs[ename].items:
                getattr(block, attr)(make(ename))


def build_program(npre=NPRE, nmain=SEG // MT, ntiles=None):
    SEG = nmain * MT
    nc = bass.Bass("TRN2", target_bir_lowering=False)
    P = Prog()
    es = ExitStack()

    def din(name, shape):
        return nc.dram_tensor(name, list(shape), F32, kind="ExternalInput").ap()

    def dout(name, shape):
        return nc.dram_tensor(name, list(shape), F32, kind="ExternalOutput").ap()

    NX = npre * MT + SEG
    xin = din("xin", [NX, D])
    xs_d = din("xs", [32, D])
    ck_d = din("ck", [128, 128])
    cv_d = din("cv", [128, 128])
    st_d = din("st", [4, 64, 128])
    hb_d = din("hb", [128, 1])
    g_mix_d = din("g_mix", [D])
    w_in_d = din("w_in", [D, 2320])
    w_alpha_d = din("w_alpha", [16, 256])
    b_alpha_d = din("b_alpha", [1, 256])
    g_q_d = din("g_q", [64, 1])
    g_k_d = din("g_k", [64, 1])
    sinks_d = din("sinks", [1, 8])
    g_gla_d = din("g_gla", [128, 1])
    w_out_d = din("w_out", [D, D])
    g_ffn_d = din("g_ffn", [D])
    w_up_d = din("w_up", [D, 4096])
    w_down_d = din("w_down", [4096, D])

    y_d = dout("y", [SEG, D])
    ys_d = dout("ys", [32, D])
    kp_d = dout("kp", [128, 128])
    vp_d = dout("vp", [128, 128])
    sp_d = dout("sp", [4, 64, 128])
    ksam_d = dout("ksam", [32, 128])
    vsam_d = dout("vsam", [32, 128])
    ssam_d = dout("ssam", [4, 64, 128])

    wstream = nc.dram_tensor("wstream", [NSLAB, 128, 1024], BF16).ap()

    def sb(name, shape, dt):
        return es.enter_context(nc.sbuf_tensor(name, list(shape), dt))

    def sem(name):
        return es.enter_context(nc.semaphore(name))

    xts = [sb(f"xt{i}", [128, 4, D], F32) for i in range(2)]
    ssq = sb("ssq", [128, 8], F32)
    rstd = sb("rstd", [128, 8], F32)
    nb = [sb(f"nb{i}", [128, D], BF16) for i in range(4)]
    nTa = sb("nTa", [128, 8, MT], BF16)
    nTb = sb("nTb", [128, 8, MT], BF16)
    qaT = sb("qaT", [128, 4, MT], BF16)
    KTz = [[sb(f"KT{i}{p}", [128, 128 + MT], BF16) for p in range(2)] for i in range(2)]
    k32T = sb("k32T", [128, 128], F32)
    qgT = sb("qgT", [128, 2, MT], F32)
    kgT = sb("kgT", [128, 2, MT], F32)
    sg = sb("sg", [128, 4, MT], F32)
    caT = sb("caT", [17, MT], BF16)
    Vlo = sb("Vlo", [128, 5, 2, 128], BF16)
    Vhi = sb("Vhi", [128, 5, 2, 128], BF16)
    kgTok = sb("kgTok", [128, 4, 256], F32)
    vgTok = sb("vgTok", [128, 4, 512], BF16)
    vouts = [sb(f"vout32_{i}", [128, 128], F32) for i in range(2)]
    kouts = [sb(f"kout32_{i}", [128, 128], F32) for i in range(2)]
    sqb = sb("sqb", [128, MT], BF16)
    rtmp = sb("rtmp", [128, MT], F32)
    PT = [sb(f"PT{i}", [128, 2, 2, 512], BF16) for i in range(1)]
    dn = sb("dn", [128, 512], F32)
    mixT = sb("mixT", [128, 8, MT], BF16)
    eu = sb("eu", [128, 1024], F32)
    Lh = sb("Lh", [128, 4, 256], BF16)
    Ll = sb("Ll", [128, 4, 256], BF16)
    E1 = sb("E1", [128, 256], F32)
    E2 = sb("E2", [128, 256], F32)
    qtT = sb("qtT", [128, 2, 128], BF16)
    ktTz = [sb(f"ktT{p}", [128, 2, 128], BF16) for p in range(2)]
    erev = eu
    kdecP = [sb(f"kdec{p}", [128, 4, 256], BF16) for p in range(2)]
    AT = sb("AT", [128, 4, 128], BF16)
    S = sb("S", [128, 2, 128], F32)
    Sbz = [sb(f"Sb{p}", [128, 2, 128], BF16) for p in range(2)]
    dec = sb("dec", [128, 8], F32)
    otmp = dn
    actT = [sb(f"actT{i}", [128, 8, MT], BF16) for i in range(2)]
    relu_t = [sb(f"relu{i}", [128, MT], F32) for i in range(2)]
    ring = sb("ring", [128, NS, 1024], BF16)
    stg_in = [sb(f"stgi{i}", [128, 1024], F32) for i in range(3)]
    stg_out = [sb(f"stgo{i}", [128, 1024], BF16) for i in range(3)]
    gm = sb("gm", [128, 8], F32)
    gf = sb("gf", [128, 8], F32)
    gq8 = sb("gq8", [128, 1], F32)
    gk1 = sb("gk1", [128, 1], F32)
    ggla = sb("ggla", [128, 1], F32)
    esink = sb("esink", [128, 4], F32)
    hb = sb("hbt", [128, 1], F32)
    wal32 = sb("wal32", [17, 256], F32)
    wal = sb("wal", [17, 256], BF16)
    triI = sb("triI", [128, 128], BF16)
    triR = sb("triR", [128, 128], BF16)
    caus = sb("caus", [128, 128], F32)
    negcol = sb("negcol", [128, 1], BF16)
    blk1 = sb("blk1", [128, 128], BF16)
    all1 = sb("all1", [128, 128], BF16)
    oneslo = sb("oneslo", [128, 128], BF16)
    oneshi = sb("oneshi", [128, 128], BF16)
    identb = sb("identb", [128, 128], BF16)
    identf = sb("identf", [128, 128], F32)
    ckd = sb("ckd", [128, 2, 128], F32)
    cv32 = sb("cv32", [128, 128], F32)

    psum_all = es.enter_context(nc.psum_tensor("psum_all", [128, 8, 512], F32))
    banks = [psum_all[:, i, :] for i in range(8)]
    U4 = psum_all[:, 6:8, :].rearrange("p a n -> p (a n)")
    RB = [Res(f"bank{i}") for i in range(8)]

    sems = {k: sem("s_" + k) for k in ("pe", "act", "dve", "pool", "sp")}

    def grp(name):
        g = P.group(name)
        g.sem = sem("g_" + name)
        return g

    g_misc = grp("misc")
    g_misc.multi = True
    g_x = [grp(f"x{i}") for i in range(4)]
    g_y = [grp(f"y{i}") for i in range(4)]
    g_out = grp("out")
    g_out.multi = True
    g_si = [grp(f"si{i}") for i in range(3)]
    for g_ in g_si:
        g_.multi = True
    g_so = [grp(f"so{i}") for i in range(3)]
    g_ring = [grp(f"r{i}") for i in range(NS)]

    block = es.enter_context(nc.Block())

    R = {}

    def res(name):
        if name not in R:
            R[name] = Res(name)
        return R[name]

    R_xts = [[res(f"xt{b}_{s}") for s in range(4)] for b in range(2)]
    R_nTa = [res(f"nTa{s}") for s in range(4)]
    R_nTb = [res(f"nTb{s}") for s in range(4)]
    R_ring = [res(f"ring{i}") for i in range(NS)]
    R_const = res("const")

    def OP(eng, fn, reads=(), writes=()):
        return P.op(eng, fn, list(reads), list(writes))

    misc_toks = []

    def MD(out, in_, w, slow=False):
        if slow:
            misc_toks.append(P.dma("pool", g_misc, lambda h: h.dma_start(out=out, in_=in_, allow_slow_non_contiguous=True), writes=[w]))
        else:
            misc_toks.append(P.dma("pool", g_misc, lambda h: h.dma_start(out=out, in_=in_), writes=[w]))

    MD(gm[:], g_mix_d.rearrange("(k p) -> p k", p=128), res("gm"), slow=True)
    MD(gf[:], g_ffn_d.rearrange("(k p) -> p k", p=128), res("gf"), slow=True)
    MD(gq8[0:64, :], g_q_d[:, :], res("gq8"))
    MD(gq8[64:128, :], g_q_d[:, :], res("gq8"))
    MD(gk1[0:64, :], g_k_d[:, :], res("gk1"))
    MD(gk1[64:128, :], g_k_d[:, :], res("gk1"))
    MD(ggla[:], g_gla_d[:, :], res("ggla"))
    sk = sinks_d.rearrange("o (i p) -> o p i", p=2)
    MD(esink[0:64, :], sk[:, 0, :].partition_broadcast(64), res("esink"), slow=True)
    MD(esink[64:128, :], sk[:, 1, :].partition_broadcast(64), res("esink"), slow=True)
    MD(hb[:], hb_d[:, :], res("hb"))
    MD(wal32[0:16, :], w_alpha_d[:, :], res("wal32"))
    MD(wal32[16:17, :], b_alpha_d[:, :], res("wal32"))
    for kvh in range(2):
        MD(ckd[:, kvh, 0:64], ck_d[:, kvh * 64:(kvh + 1) * 64], res("ckd"))
        MD(ckd[:, kvh, 64:128], ck_d[:, kvh * 64:(kvh + 1) * 64], res("ckd"))
    MD(cv32[:], cv_d[:, :], res("cv32"))

    misc_final = ("d", g_misc, g_misc.count)
    for nm in ("gm", "gf", "gq8", "gk1", "ggla", "esink", "hb", "wal32", "ckd", "cv32"):
        res(nm).w = misc_final
    OP("act", lambda h: h.mul(out=gq8[:], in_=gq8[:], mul=0.125), [res("gq8")], [res("gq8")])
    OP("act", lambda h: h.activation(out=esink[:], in_=esink[:], func=AF.Exp), [res("esink")], [res("esink")])
    OP("act", lambda h: h.copy(out=wal[:], in_=wal32[:]), [res("wal32")], [R_const])
    OP("pool", lambda h: h.memset(identf[:], 0.0), [], [res("identf")])
    OP("pool", lambda h: h.affine_select(out=identf[:], in_=identf[:], pattern=[[-1, 128]], compare_op=ALU.not_equal,
                                          fill=1.0, base=0, channel_multiplier=1), [res("identf")], [res("identf")])
    OP("pool", lambda h: h.tensor_copy(out=identb[:], in_=identf[:]), [res("identf")], [R_const])
    OP("pool", lambda h: h.memset(triI[:], -1.0 / 16), [], [res("triI")])
    OP("pool", lambda h: h.affine_select(out=triI[:], in_=triI[:], pattern=[[1, 128]], compare_op=ALU.is_ge,
                                          fill=0.0, base=0, channel_multiplier=-1), [res("triI")], [res("triI")])
    OP("pool", lambda h: h.memset(triR[:], -1.0 / 16), [], [res("triR")])
    OP("pool", lambda h: h.affine_select(out=triR[:], in_=triR[:], pattern=[[-1, 128]], compare_op=ALU.is_gt,
                                          fill=0.0, base=0, channel_multiplier=1), [res("triR")], [res("triR")])
    OP("pool", lambda h: h.memset(caus[:], 1.0), [], [res("caus")])
    OP("pool", lambda h: h.affine_select(out=caus[:], in_=caus[:], pattern=[[1, 128]], compare_op=ALU.is_ge,
                                          fill=0.0, base=0, channel_multiplier=-1), [res("caus")], [res("caus")])
    OP("pool", lambda h: h.memset(negcol[:], -1.0 / 16), [], [R_const])
    OP("pool", lambda h: h.memset(all1[:], 1.0), [], [R_const])
    OP("pool", lambda h: h.memset(blk1[:], 1.0), [], [res("blk1")])
    OP("pool", lambda h: h.memset(blk1[0:64, 64:128], 0.0), [res("blk1")], [res("blk1")])
    OP("pool", lambda h: h.memset(blk1[64:128, 0:64], 0.0), [res("blk1")], [res("blk1")])
    OP("pool", lambda h: h.memset(oneslo[:], 0.0), [], [res("oneslo")])
    OP("pool", lambda h: h.memset(oneslo[:, 0:64], 1.0), [res("oneslo")], [res("oneslo")])
    OP("pool", lambda h: h.memset(oneshi[:], 0.0), [], [res("oneshi")])
    OP("pool", lambda h: h.memset(oneshi[:, 64:128], 1.0), [res("oneshi")], [res("oneshi")])
    OP("pool", lambda h: h.memset(caT[:], 1.0), [], [res("caT")])
    OP("pool", lambda h: h.memset(Vlo[:], 0.0), [], [res("V")])
    OP("pool", lambda h: h.memset(Vhi[:], 0.0), [res("V")], [res("V")])
    OP("pool", lambda h: h.memset(S[:], 0.0), [], [res("S")])
    for p in range(2):
        OP("pool", lambda h, p=p: h.memset(kdecP[p][:], 0.0), [], [res("kdec")])
    for p in range(2):
        OP("pool", lambda h, p=p: h.memset(Sbz[p][:], 0.0), [], [res("Sb")])
        OP("pool", lambda h, p=p: h.memset(ktTz[p][:], 0.0), [], [res("ktT")])
        for kvh in range(2):
            OP("pool", lambda h, p=p, kvh=kvh: h.memset(KTz[kvh][p][:], 0.0), [], [res(f"KT{kvh}")])
    for i in range(1):
        OP("pool", lambda h, i=i: h.memset(PT[i][:], 0.0), [], [res(f"PT{i}")])
    for i in range(3):
        OP("pool", lambda h, i=i: h.memset(stg_out[i][:], 0.0), [], [res(f"stgo{i}")])
    CONSTS = [R_const, res("identf"), res("triI"), res("triR"), res("caus"), res("blk1"), res("oneslo"), res("oneshi"),
              res("gm"), res("gf"), res("gq8"), res("gk1"), res("ggla"), res("esink"), res("hb")]

    prep_jobs = {}
    prep_state = {"loaded": 0, "done": 0, "order": [], "prepped": set()}

    def prep_slab(sid, srcs, conv):
        prep_jobs[sid] = (srcs, conv)

    def prep_load(k):
        sid = prep_state["order"][k]
        i = k % 3
        srcs, _ = prep_jobs[sid]
        for (dfn, src) in srcs:
            P.dma("sp", g_si[i], lambda h, dfn=dfn, src=src, i=i: h.dma_start(out=dfn(stg_in[i]), in_=src), writes=[res(f"stgi{i}")])

    def prep_finish(k):
        sid = prep_state["order"][k]
        i = k % 3
        ri, ro = res(f"stgi{i}"), res(f"stgo{i}")
        _, (eng, fn) = prep_jobs[sid]
        OP(eng, lambda h, fn=fn, i=i: fn(h, stg_out[i], stg_in[i]), [ri] + CONSTS[8:10], [ro])
        P.dma("sp", g_so[i], lambda h, i=i, sid=sid: h.dma_start(out=wstream[sid], in_=stg_out[i][:]), reads=[ro], writes=[res(f"ws{sid}")])
        prep_state["prepped"].add(sid)

    def prep_step():
        k = prep_state["done"]
        n = len(prep_state["order"])
        if k >= n:
            return False
        while prep_state["loaded"] < min(n, k + 3):
            prep_load(prep_state["loaded"])
            prep_state["loaded"] += 1
        prep_finish(k)
        prep_state["done"] += 1
        return True

    def ensure_prepped(sid):
        while sid not in prep_state["prepped"]:
            assert prep_step()

    w_in_k = w_in_d.rearrange("(k p) n -> p k n", p=128)
    w_up_k = w_up_d.rearrange("(k p) n -> p k n", p=128)

    def v3(t, w=128):
        return t[:].rearrange("p (k n) -> p k n", k=8)[:, :, 0:w]

    def convA(gt, cs, w, engname):
        if cs != 1.0:
            engname = "dve"

        def fn(h, o, i):
            if engname == "dve":
                return h.scalar_tensor_tensor(out=v3(o, w), in0=v3(i, w), scalar=cs,
                                              in1=gt[:].unsqueeze(2).to_broadcast([128, 8, w]), op0=ALU.mult, op1=ALU.mult)
            return h.tensor_tensor(out=v3(o, w), in0=v3(i, w), in1=gt[:].unsqueeze(2).to_broadcast([128, 8, w]), op=ALU.mult)
        return (engname, fn)

    eng_rr = ["dve", "dve"]
    rr = [0]

    def next_eng():
        rr[0] += 1
        return eng_rr[rr[0] % 2]

    a_cols = [(0, 128), (128, 128), (256, 128), (384, 128),
              None, None,
              (768, 128), (896, 128),
              (1024, 128), (1152, 128),
              (1792, 128), (1920, 128), (2048, 128), (2176, 128),
              (2304, 16)]
    for ci, cw in enumerate(a_cols):
        if cw is None:
            kvh = ci - A_KA
            c0 = 512 + kvh * 64
            srcs = [(lambda t: t[:].rearrange("p (k n) -> p k n", k=8)[:, :, 0:64], w_in_k[:, :, c0:c0 + 64]),
                    (lambda t: t[:].rearrange("p (k n) -> p k n", k=8)[:, :, 64:128], w_in_k[:, :, c0:c0 + 64])]
            prep_slab(ci, srcs, convA(gm, 1.0, 128, next_eng()))
        else:
            c0, w = cw
            srcs = [(lambda t, w=w: v3(t, w), w_in_k[:, :, c0:c0 + w])]
            cs = 0.125 if ci in (A_QG, A_QG + 1) else 1.0
            prep_slab(ci, srcs, convA(gm, cs, w, next_eng()))
    for kc in range(8):
        rows = w_in_d[kc * 128:(kc + 1) * 128, :]
        srcs = [(lambda t: t[:, 0:512], rows[:, 1280:1792]),
                (lambda t: t[:, 512:640], rows[:, 640:768]),
                (lambda t: t[:, 640:896], rows[:, 1024:1280])]
        prep_slab(B0 + kc, srcs, ("act", lambda h, o, i, kc=kc: h.activation(out=o[:, 0:896], in_=i[:, 0:896], func=AF.Copy,
                                                                           scale=gm[:, kc:kc + 1])))
    for kc in range(8):
        srcs = [(lambda t: t[:], w_out_d[kc * 128:(kc + 1) * 128, :])]
        e = ["act", "dve"][kc % 2]
        if e == "act":
            prep_slab(O0 + kc, srcs, ("act", lambda h, o, i: h.copy(out=o[:], in_=i[:])))
        else:
            prep_slab(O0 + kc, srcs, (e, lambda h, o, i: h.tensor_copy(out=o[:], in_=i[:])))
    for f in range(32):
        srcs = [(lambda t: v3(t, 128), w_up_k[:, :, f * 128:(f + 1) * 128])]
        prep_slab(U0 + f, srcs, convA(gf, 1.0, 128, next_eng()))
    for f in range(32):
        srcs = [(lambda t: t[:], w_down_d[f * 128:(f + 1) * 128, :])]
        e = ["act", "dve"][f % 2]
        if e == "act":
            prep_slab(D0 + f, srcs, ("act", lambda h, o, i: h.copy(out=o[:], in_=i[:])))
        else:
            prep_slab(D0 + f, srcs, (e, lambda h, o, i: h.tensor_copy(out=o[:], in_=i[:])))
    prep_state["order"] = ([A_CA] + list(range(B0, B0 + 8)) + [A_KA, A_KA + 1] + [0, 10, 1, 11, 2, 12, 3, 13, 6, 8, 7, 9]
                           + list(range(O0, O0 + 8)))
    for G in range(4):
        prep_state["order"] += list(range(U0 + 8 * G, U0 + 8 * G + 8)) + list(range(D0 + 8 * G, D0 + 8 * G + 8))
    assert sorted(prep_state["order"]) == list(range(NSLAB))

    stream = []
    st_state = {"declared": 0, "released": 0, "flat": [], "gstart": []}

    def plan(groups):
        for g in groups:
            st_state["gstart"].append(len(st_state["flat"]))
            st_state["flat"].extend(g)
            stream.append(g)

    def pump():
        flat = st_state["flat"]
        while st_state["declared"] < len(flat) and st_state["declared"] < st_state["released"] + NS:
            n = st_state["declared"]
            slot = n % NS
            sid = flat[n]
            ensure_prepped(sid)
            P.dma("sp", g_ring[slot], lambda h, slot=slot, sid=sid: h.dma_start(out=ring[:, slot, :], in_=wstream[sid]),
                  reads=[res(f"ws{sid}")], writes=[R_ring[slot]])
            st_state["declared"] += 1

    gi = [0]

    def acquire():
        g = stream[gi[0]]
        start = st_state["gstart"][gi[0]]
        assert start == st_state["released"], (start, st_state["released"])
        assert start + len(g) <= st_state["released"] + NS
        pump()
        assert st_state["declared"] >= start + len(g)
        out = []
        for k in range(len(g)):
            slot = (start + k) % NS
            out.append((slot, R_ring[slot]))
        return out

    def release():
        g = stream[gi[0]]
        st_state["released"] += len(g)
        gi[0] += 1
        pump()

    A_MAIN = [0, 10, 1, 11, 2, 12, 3, 13, 4, 6, 5, 8, 7, 9, 14]

    def a_order(mode):
        return {"halo": [A_KA, A_KA + 1], "pre": [A_CA]}.get(mode, A_MAIN)

    def tile_groups(mode):
        if mode == "halo":
            return [[A_KA], [A_KA + 1], list(range(B0, B0 + 8))]
        if mode == "pre":
            return [[A_CA], list(range(B0, B0 + 8))]
        gs = [[i] for i in A_MAIN] + [list(range(B0, B0 + 8))] + [list(range(O0, O0 + 8))]
        for G in range(4):
            gs += [[U0 + 8 * G + j] for j in range(8)]
            gs += [list(range(D0 + 8 * G, D0 + 8 * G + 8))]
        return gs

    tiles = [("pre", i * MT) for i in range(npre)] + [("halo", npre * MT - 128)] + \
            [("main", npre * MT + i * MT) for i in range(SEG // MT)] + [("sample", 0)]
    if ntiles is not None:
        tiles = tiles[:ntiles]
    for (mode, _) in tiles:
        plan(tile_groups(mode))

    y_toks = []
    out_toks = []
    nbi = [0]
    pti = [0]
    acti = [0]

    def bf_view(bank):
        return banks[bank][:].bitcast(BF16).rearrange("p (k t) -> p k t", k=8)

    hook_holder = [None]

    def fire(name):
        if hook_holder[0] is not None:
            hook_holder[0](name)

    def run_tile(mode, tok0, buf, first_main=False, last_main=False):
        vout32 = vouts[1 if mode == "sample" else 0]
        kout32 = kouts[1 if mode == "sample" else 0]
        rvo = res("vout32_%d" % (1 if mode == "sample" else 0))
        rko = res("kout32_%d" % (1 if mode == "sample" else 0))
        xt = xts[buf]
        R_xt = R_xts[buf]
        sample = mode == "sample"
        TS = 32 if sample else 128
        nsub = 1 if (sample or mode == "halo") else 4
        NT = TS * nsub
        src = xs_d if sample else xin

        for s in range(nsub):
            P.dma("pool", g_x[s], lambda h, s=s: h.dma_start(out=xt[:TS, s, :], in_=src[tok0 + s * TS: tok0 + (s + 1) * TS, :]),
                  writes=[R_xt[s]])

        yield "load"

        def norm_A(col0, s):
            b = s
            rnb = res(f"nb{b}")
            c = col0 + s
            rq, rr2 = res(f"ssq{c}"), res(f"rstd{c}")
            OP("act", lambda h, s=s, b=b, c=c: h.activation(out=nb[b][:TS, :], in_=xt[:TS, s, :], func=AF.Square,
                                                          accum_out=ssq[:TS, c:c + 1]), [R_xt[s]], [rnb, rq])
            OP("act", lambda h, c=c: h.activation(out=rstd[:TS, c:c + 1], in_=ssq[:TS, c:c + 1], func=AF.Ln,
                                                  scale=1.0 / D, bias=EPS), [rq], [rr2])
            OP("act", lambda h, c=c: h.activation(out=rstd[:TS, c:c + 1], in_=rstd[:TS, c:c + 1], func=AF.Exp,
                                                  scale=-0.5), [rr2], [rr2])
            OP("dve", lambda h, s=s, b=b, c=c: h.tensor_scalar_mul(out=nb[b][:TS, :], in0=xt[:TS, s, :], scalar1=rstd[:TS, c:c + 1]),
               [R_xt[s], rr2], [rnb])

        def norm_B(s, nT, R_nT):
            b = s
            rnb = res(f"nb{b}")
            tp = bf_view(3)
            for kc in range(8):
                OP("pe", lambda h, kc=kc, b=b, tp=tp: h.transpose(out=tp[:, kc, :TS], in_=nb[b][:TS, kc * 128:(kc + 1) * 128],
                                                                identity=identb[:TS, :TS]), [rnb, R_const], [RB[3]])
            OP("dve", lambda h, s=s, tp=tp, nT=nT: h.tensor_copy(out=nT[:, :, s * TS:(s + 1) * TS], in_=tp[:, :, :TS]), [RB[3]], [R_nT[s]])

        def norm_T(col0, nT, R_nT):
            for s in range(nsub):
                norm_A(col0, s)
                norm_B(s, nT, R_nT)

        for s in range(nsub):
            norm_A(0, s)
            yield "normA"
        for s in range(nsub):
            norm_B(s, nTa, R_nTa)
        yield "front"
        nT, R_nT = nTa, R_nTa
        RnT = R_nT[:nsub]

        mmb = [0]

        def proj_chunk(M, evac):
            (slot, rr_), = acquire()
            bk = (0, 1, 6, 7)[mmb[0] % 4]
            mmb[0] += 1
            for kc in range(8):
                OP("pe", lambda h, kc=kc, slot=slot, bk=bk: h.matmul(banks[bk][:M, :NT], lhsT=ring[:, slot, kc * 128:kc * 128 + M],
                                                                   rhs=nTa[:, kc, :NT], start=(kc == 0), stop=(kc == 7)),
                   [rr_] + RnT, [RB[bk]])
            release()
            flush_pending()
            r = evac(bk)
            if callable(r):
                pending.append(r)

        pending = []

        def flush_pending():
            for f_ in pending:
                f_()
            del pending[:]

        def qknorm_evac(gvec, gres, outs):
            def ev(bk):
                OP("act", lambda h: h.activation(out=sqb[:, :NT], in_=banks[bk][:, :NT], func=AF.Square), [RB[bk]], [res("sqb")])
                return lambda: stage2(bk)

            def stage2(bk):
                OP("pe", lambda h: h.matmul(banks[2][:, :NT], lhsT=blk1[:], rhs=sqb[:, :NT], start=True, stop=True),
                   [res("sqb"), res("blk1")], [RB[2]])
                OP("act", lambda h: h.activation(out=rtmp[:, :NT], in_=banks[2][:, :NT], func=AF.Ln, scale=1.0 / 64, bias=EPS),
                   [RB[2]], [res("rtmp")])
                OP("act", lambda h: h.activation(out=rtmp[:, :NT], in_=rtmp[:, :NT], func=AF.Exp, scale=-0.5),
                   [res("rtmp")], [res("rtmp")])
                for (p0, p1, dfn, dres) in outs:
                    OP("dve", lambda h, dfn=dfn, p0=p0, p1=p1: h.scalar_tensor_tensor(out=dfn(), in0=banks[bk][p0:p1, :NT], scalar=gvec[p0:p1, 0:1],
                                                                       in1=rtmp[p0:p1, :NT], op0=ALU.mult, op1=ALU.mult),
                       [RB[bk], res("rtmp"), gres], [dres])
            return ev

        koff = 0 if mode == "halo" else 128
        want_kout = sample or last_main

        def chunk_qa(i):
            proj_chunk(128, qknorm_evac(gq8, res("gq8"), [(0, 128, lambda i=i: qaT[:, i, :NT], res("qaT"))]))

        def chunk_ka(kvh):
            outs = [(p * 64, (p + 1) * 64, lambda kvh=kvh, p=p: KTz[kvh][p][p * 64:(p + 1) * 64, koff:koff + NT], res(f"KT{kvh}")) for p in range(2)]
            if want_kout:
                def ev(bk, kvh=kvh, outs=outs):
                    base = qknorm_evac(gk1, res("gk1"), outs)
                    st2 = base(bk)
                    return lambda: (st2(), k32_extra(bk, kvh))

                def k32_extra(bk, kvh):
                    OP("dve", lambda h: h.scalar_tensor_tensor(out=k32T[kvh * 64:(kvh + 1) * 64, 0:TS],
                                                               in0=banks[bk][kvh * 64:(kvh + 1) * 64, NT - TS:NT], scalar=gk1[kvh * 64:(kvh + 1) * 64, 0:1],
                                                               in1=rtmp[kvh * 64:(kvh + 1) * 64, NT - TS:NT], op0=ALU.mult, op1=ALU.mult),
                       [RB[bk], res("rtmp"), res("gk1")], [res("k32T")])
                proj_chunk(128, ev)
            else:
                proj_chunk(128, qknorm_evac(gk1, res("gk1"), outs))

        def chunk_qg(i):
            proj_chunk(128, lambda bk, i=i: OP("act", lambda h: h.copy(out=qgT[:, i, :NT], in_=banks[bk][:, :NT]), [RB[bk]], [res("qgT")]))

        def chunk_kg(i):
            proj_chunk(128, lambda bk, i=i: OP("act", lambda h: h.copy(out=kgT[:, i, :NT], in_=banks[bk][:, :NT]), [RB[bk]], [res("kgT")]))

        def chunk_rg(i):
            def ev(bk, i=i):
                OP("act", lambda h: h.activation(out=sg[:, i, :NT], in_=banks[bk][:, :NT], func=AF.Exp, scale=-1.0), [RB[bk]], [res(f"sg{i}")])
                OP("dve", lambda h: h.tensor_scalar_add(out=sg[:, i, :NT], in0=sg[:, i, :NT], scalar1=1.0), [res(f"sg{i}")], [res(f"sg{i}")])
                OP("dve", lambda h: h.reciprocal(out=sg[:, i, :NT], in_=sg[:, i, :NT]), [res(f"sg{i}")], [res(f"sg{i}")])
                OP("dve", lambda h: h.tensor_tensor(out=sg[:, i, :NT], in0=banks[bk][:, :NT], in1=sg[:, i, :NT], op=ALU.mult),
                   [RB[bk], res(f"sg{i}")], [res(f"sg{i}")])
            proj_chunk(128, ev)

        def chunk_ca():
            proj_chunk(128, lambda bk: OP("act", lambda h: h.copy(out=caT[0:16, :NT], in_=banks[bk][0:16, :NT]), [RB[bk]], [res("caT")]))

        for cid in a_order(mode):
            if cid < 4:
                chunk_qa(cid)
            elif cid < 6:
                chunk_ka(cid - 4)
            elif cid < 8:
                chunk_qg(cid - 6)
            elif cid < 10:
                chunk_kg(cid - 8)
            elif cid < 14:
                chunk_rg(cid - 10)
            else:
                chunk_ca()

        flush_pending()
        slots = acquire()
        for s in range(nsub):
            for (bk, c0, cw) in ((4, 0, 512), (5, 640, 256) if mode == "pre" else (5, 512, 384)):
                for kc in range(8):
                    slot, rr_ = slots[kc]
                    OP("pe", lambda h, kc=kc, slot=slot, bk=bk, c0=c0, cw=cw, s=s: h.matmul(
                        banks[bk][:TS, :cw], lhsT=nTa[:, kc, s * TS:(s + 1) * TS], rhs=ring[:, slot, c0:c0 + cw],
                        start=(kc == 0), stop=(kc == 7)), [rr_, R_nT[s]], [RB[bk]])
            if mode != "halo":
                OP("act", lambda h, s=s: h.copy(out=vgTok[:TS, s, :], in_=banks[4][:TS, :]), [RB[4]], [res("vgTok")])
                kg0 = 0 if mode == "pre" else 128
                OP("act", lambda h, s=s, kg0=kg0: h.copy(out=kgTok[:TS, s, :], in_=banks[5][:TS, kg0:kg0 + 256]), [RB[5]], [res("kgTok")])
            if mode != "pre":
                vt = 0 if mode == "halo" else s + 1
                OP("act", lambda h, vt=vt: h.copy(out=Vlo[:TS, vt, :, 0:64], in_=banks[5][:TS, 0:128].rearrange("p (a d) -> p a d", a=2)),
                   [RB[5]], [res("V")])
                OP("act", lambda h, vt=vt: h.copy(out=Vhi[:TS, vt, :, 64:128], in_=banks[5][:TS, 0:128].rearrange("p (a d) -> p a d", a=2)),
                   [RB[5]], [res("V")])
                if want_kout and s == nsub - 1:
                    OP("act", lambda h: h.copy(out=vout32[:TS, :], in_=banks[5][:TS, 0:128]), [RB[5]], [rvo])
            if mode == "pre":
                fire("Bsub")
        release()
        if mode == "halo":
            yield "mid"
            return

        def gla_batch():
            W = 256 * nsub
            R67 = [RB[6], RB[7]]
            for s in range(nsub):
                OP("pe", lambda h, s=s: h.matmul(U4[:TS, s * 256:(s + 1) * 256], lhsT=caT[0:17, s * TS:(s + 1) * TS], rhs=wal[0:17, :],
                                                 start=True, stop=True), [res("caT"), R_const], R67)
            OP("act", lambda h: h.activation(out=eu[:TS, :W], in_=U4[:TS, :W], func=AF.Exp, scale=-1.0), R67, [res("eu")])
            if mode == "pre":
                fire("after_rev")
            OP("act", lambda h: h.activation(out=eu[:TS, :W], in_=eu[:TS, :W], func=AF.Ln, bias=1.0), [res("eu")], [res("eu")])
            euv = eu[:TS, :W].rearrange("p (a n) -> p a n", a=nsub)
            OP("act", lambda h: h.copy(out=Lh[:TS, 0:nsub, :], in_=euv), [res("eu")], [res("Lt")])
            OP("dve", lambda h: h.tensor_tensor(out=Ll[:TS, 0:nsub, :], in0=euv, in1=Lh[:TS, 0:nsub, :], op=ALU.subtract),
               [res("eu"), res("Lt")], [res("Lt")])
            for s in range(nsub):
                for k_, Lx in enumerate((Lh, Ll)):
                    OP("pe", lambda h, s=s, k_=k_, Lx=Lx: h.matmul(U4[:TS, s * 256:(s + 1) * 256], lhsT=triR[:TS, :TS], rhs=Lx[:TS, s, :],
                                                                 start=(k_ == 0), stop=(k_ == 1)), [res("Lt"), res("triR")], R67)
                for i in range(2):
                    for k_, Lx in enumerate((Lh, Ll)):
                        OP("pe", lambda h, s=s, i=i, k_=k_, Lx=Lx: h.matmul(banks[2][:, 2 * s + i:2 * s + i + 1], lhsT=Lx[:TS, s, i * 128:(i + 1) * 128],
                                                                          rhs=negcol[:TS, 0:1], start=(k_ == 0), stop=(k_ == 1)),
                           [res("Lt"), R_const], [RB[2]])
            OP("act", lambda h: h.activation(out=erev[:TS, :W], in_=U4[:TS, :W], func=AF.Exp), R67, [res("eu")])
            OP("act", lambda h: h.activation(out=dec[:, 0:2 * nsub], in_=banks[2][:, 0:2 * nsub], func=AF.Exp), [RB[2]], [res("dec")])
            for p in range(2):
                v5 = lambda t, p=p: t.rearrange("q a (i r d) -> q a i r d", i=2, r=2)[:TS, 0:nsub, :, p, :]
                OP("dve", lambda h, p=p, v5=v5: h.tensor_tensor(out=v5(kdecP[p][:]), in0=v5(kgTok[:]),
                                                               in1=v5(erev[:, 0:1024].rearrange("q (a n) -> q a n", a=4)), op=ALU.mult),
                   [res("kgTok"), res("eu")], [res("kdec")])

        def gla_state(s):
            sbk = (6 + s % 2) if mode == "pre" else 3
            for i in range(2):
                for p in range(2):
                    OP("pe", lambda h, i=i, p=p: h.matmul(banks[sbk][:, i * 128:(i + 1) * 128], lhsT=kdecP[p][:TS, s, i * 128:(i + 1) * 128],
                                                         rhs=vgTok[:TS, s, (2 * i + p) * 128:(2 * i + p + 1) * 128], start=(p == 0), stop=(p == 1)),
                       [res("kdec"), res("vgTok")], [RB[sbk]])
            for i in range(2):
                OP("dve", lambda h, i=i: h.scalar_tensor_tensor(
                    out=S[:, i, :], in0=S[:, i, :], scalar=dec[:, 2 * s + i:2 * s + i + 1],
                    in1=banks[sbk][:, i * 128:(i + 1) * 128], op0=ALU.mult, op1=ALU.add),
                   [res("S"), res("dec"), RB[sbk]], [res("S")])

        if mode == "pre":
            gla_batch()
            for s in range(nsub):
                gla_state(s)
            yield "mid"
            return

        if sample:
            for kvh in range(2):
                OP("pe", lambda h, kvh=kvh: h.transpose(out=banks[7][:, kvh * 128:(kvh + 1) * 128], in_=ckd[:, kvh, :], identity=identf[:]),
                   [res("ckd"), res("identf")], [RB[7]])
                for p in range(2):
                    OP("act", lambda h, kvh=kvh, p=p: h.copy(out=KTz[kvh][p][p * 64:(p + 1) * 64, 0:128],
                                                             in_=banks[7][p * 64:(p + 1) * 64, kvh * 128:(kvh + 1) * 128]), [RB[7]], [res(f"KT{kvh}")])
            OP("dve", lambda h: h.tensor_copy(out=Vlo[:, 0, :, 0:64], in_=cv32[:].rearrange("p (a d) -> p a d", a=2)), [res("cv32")], [res("V")])
            OP("dve", lambda h: h.tensor_copy(out=Vhi[:, 0, :, 64:128], in_=cv32[:].rearrange("p (a d) -> p a d", a=2)), [res("cv32")], [res("V")])
            for hh in range(4):
                P.dma("pool", g_misc, lambda h, hh=hh: h.dma_start(out=S[(hh % 2) * 64:(hh % 2 + 1) * 64, hh // 2, :], in_=st_d[hh]),
                      writes=[res("S")])
            for p in range(2):
                OP("act", lambda h, p=p: h.copy(out=Sbz[p][p * 64:(p + 1) * 64, :, :], in_=S[p * 64:(p + 1) * 64, :, :]), [res("S")], [res("Sb")])

        if first_main:
            for p in range(2):
                OP("act", lambda h, p=p: h.copy(out=Sbz[p][p * 64:(p + 1) * 64, :, :], in_=S[p * 64:(p + 1) * 64, :, :]), [res("S")], [res("Sb")])

        def attention(s):
            pb = 0
            pti[0] += 1
            rPT = res(f"PT{pb}")
            sbank = {(0, 0): 0, (0, 1): 1, (1, 0): 4, (1, 1): 5}
            nk = [128, TS]
            for kvh in range(2):
                for kt in range(2):
                    bk = sbank[(kt, kvh)]
                    kcol = s * 128 if kt == 0 else 128 + s * TS
                    for par in range(2):
                        for cl in range(2):
                            OP("pe", lambda h, kvh=kvh, kt=kt, bk=bk, kcol=kcol, par=par, cl=cl: h.matmul(
                                banks[bk][:nk[kt], (par * 2 + cl) * TS:(par * 2 + cl + 1) * TS],
                                lhsT=KTz[kvh][par][:, kcol:kcol + nk[kt]],
                                rhs=qaT[:, 2 * kvh + cl, s * TS:(s + 1) * TS],
                                start=True, stop=True), [res(f"KT{kvh}"), res("qaT")], [RB[bk]])
                    src4 = banks[bk][:, 0:4 * TS].rearrange("p (a q) -> p a q", a=4)
                    dst4 = PT[pb][:, kt, kvh, 0:4 * TS].rearrange("p (a q) -> p a q", a=4)
                    if sample:
                        regs = [(0, nk[kt], 0, TS)]
                    elif kt == 0:
                        regs = [(0, 64, 0, 64), (64, 128, 0, 128)]
                    else:
                        regs = [(0, 64, 0, 128), (64, 128, 64, 128)]
                    for (p0, p1, q0, q1) in regs:
                        if first_main and s == 0 and kt == 0:
                            OP("act", lambda h, p0=p0, p1=p1, q0=q0, q1=q1, src4=src4, dst4=dst4: h.activation(
                                out=dst4[p0:p1, :, q0:q1], in_=src4[p0:p1, :, q0:q1], func=AF.Exp, bias=hb[p0:p1, 0:1]),
                               [RB[bk], res("hb")], [rPT])
                        else:
                            OP("act", lambda h, p0=p0, p1=p1, q0=q0, q1=q1, src4=src4, dst4=dst4: h.activation(
                                out=dst4[p0:p1, :, q0:q1], in_=src4[p0:p1, :, q0:q1], func=AF.Exp), [RB[bk]], [rPT])
        def att_B(s):
            pb = 0
            rPT = res(f"PT{pb}")
            nk = [128, TS]
            for (bk, lo, hi, rl) in ((6, None, None, "V"), (7, oneslo, oneshi, None)):
                for i in range(4):
                    kvh = i // 2
                    cl = i % 2
                    n = 0
                    for kt in range(2):
                        vt = s if kt == 0 else s + 1
                        for par in range(2):
                            if lo is None:
                                lhs = (Vlo if par == 0 else Vhi)[:nk[kt], vt, kvh, :]
                                rd = [res("V"), rPT]
                            else:
                                lhs = (lo if par == 0 else hi)[:nk[kt], :]
                                rd = [res("oneslo"), res("oneshi"), rPT]
                            col = (par * 2 + cl) * TS
                            OP("pe", lambda h, bk=bk, i=i, lhs=lhs, kt=kt, kvh=kvh, col=col, n=n: h.matmul(
                                banks[bk][:, i * TS:(i + 1) * TS], lhsT=lhs, rhs=PT[pb][:nk[kt], kt, kvh, col:col + TS],
                                start=(n == 0), stop=(n == 3)), rd, [RB[bk]])
                            n += 1
        def att_C(s):
            W4 = 4 * TS
            OP("dve", lambda h: h.tensor_tensor(out=dn[:, 0:W4].rearrange("p (a q) -> p a q", a=4),
                                                in0=banks[7][:, 0:W4].rearrange("p (a q) -> p a q", a=4),
                                                in1=esink[:].unsqueeze(2).to_broadcast([128, 4, TS]), op=ALU.add),
               [RB[7], res("esink")], [res("dn")])
            OP("dve", lambda h: h.reciprocal(out=dn[:, 0:W4], in_=dn[:, 0:W4]), [res("dn")], [res("dn")])
            OP("dve", lambda h: h.tensor_tensor(out=mixT[:, 0:4, s * TS:(s + 1) * TS], in0=banks[6][:, 0:W4].rearrange("p (a q) -> p a q", a=4),
                                                in1=dn[:, 0:W4].rearrange("p (a q) -> p a q", a=4), op=ALU.mult),
               [RB[6], res("dn")], [res(f"mixA{s}")])

        def g1(s):
            for i in range(2):
                for k_, Lx in enumerate((Lh, Ll)):
                    OP("pe", lambda h, i=i, k_=k_, Lx=Lx: h.matmul(banks[2][:, i * TS:(i + 1) * TS], lhsT=Lx[:TS, s, i * 128:(i + 1) * 128],
                                                                 rhs=triI[:TS, :TS], start=(k_ == 0), stop=(k_ == 1)), [res("Lt"), res("triI")], [RB[2]])
            W2 = 2 * TS
            OP("act", lambda h: h.activation(out=E1[:, 0:W2], in_=banks[2][:, 0:W2], func=AF.Exp), [RB[2]], [res("E1")])
            OP("act", lambda h: h.activation(out=E2[:, 0:W2], in_=banks[2][:, 0:W2], func=AF.Exp, scale=-1.0), [RB[2]], [res("E2")])
            OP("dve", lambda h: h.tensor_tensor(out=qtT[:, :, :TS], in0=qgT[:, :, s * TS:(s + 1) * TS],
                                                in1=E1[:, 0:W2].rearrange("p (a t) -> p a t", a=2), op=ALU.mult),
               [res("qgT"), res("E1")], [res("qtT")])
            for p in range(2):
                OP("dve", lambda h, p=p: h.tensor_tensor(out=ktTz[p][p * 64:(p + 1) * 64, :, :TS], in0=kgT[p * 64:(p + 1) * 64, :, s * TS:(s + 1) * TS],
                                                         in1=E2[p * 64:(p + 1) * 64, 0:W2].rearrange("p (a t) -> p a t", a=2), op=ALU.mult),
                   [res("kgT"), res("E2")], [res("ktT")])
        def g2(s):
            for hh in range(4):
                p, i = hh % 2, hh // 2
                OP("pe", lambda h, hh=hh, p=p, i=i: h.matmul(banks[3][:TS, hh * TS:(hh + 1) * TS], lhsT=ktTz[p][:, i, :TS],
                                                           rhs=qtT[:, i, :TS], start=True, stop=True),
                   [res("ktT"), res("qtT")], [RB[3]])
            W4 = 4 * TS
            OP("dve", lambda h: h.tensor_tensor(out=AT[:TS, :, :TS], in0=banks[3][:TS, 0:W4].rearrange("p (a t) -> p a t", a=4),
                                                in1=caus[:TS, :TS].unsqueeze(1).to_broadcast([TS, 4, TS]), op=ALU.mult),
               [RB[3], res("caus")], [res("AT")])

        def g3(s):
            for hh in range(4):
                p, i = hh % 2, hh // 2
                OP("pe", lambda h, hh=hh: h.matmul(banks[2][:, hh * TS:(hh + 1) * TS], lhsT=vgTok[:TS, s, hh * 128:(hh + 1) * 128],
                                                   rhs=AT[:TS, hh, :TS], start=True, stop=False), [res("vgTok"), res("AT")], [RB[2]])
                OP("pe", lambda h, hh=hh, p=p, i=i: h.matmul(banks[2][:, hh * TS:(hh + 1) * TS], lhsT=Sbz[p][:, i, :],
                                                           rhs=qtT[:, i, :TS], start=False, stop=True),
                   [res("Sb"), res("qtT")], [RB[2]])

        def g4(s):
            gla_state(s)
            for p in range(2):
                OP("act", lambda h, p=p: h.copy(out=Sbz[p][p * 64:(p + 1) * 64, :, :], in_=S[p * 64:(p + 1) * 64, :, :]), [res("S")], [res("Sb")])

        def g5(s):
            W4 = 4 * TS
            OP("act", lambda h: h.activation(out=sqb[:, :W4], in_=banks[2][:, :W4], func=AF.Square), [RB[2]], [res("sqb")])
            OP("pe", lambda h: h.matmul(banks[3][:, :W4], lhsT=all1[:], rhs=sqb[:, :W4], start=True, stop=True), [res("sqb"), R_const], [RB[3]])
            OP("act", lambda h: h.activation(out=rtmp[:, :W4], in_=banks[3][:, :W4], func=AF.Ln, scale=1.0 / 128, bias=EPS), [RB[3]], [res("rtmp")])
            OP("act", lambda h: h.activation(out=rtmp[:, :W4], in_=rtmp[:, :W4], func=AF.Exp, scale=-0.5), [res("rtmp")], [res("rtmp")])
            OP("dve", lambda h: h.scalar_tensor_tensor(out=otmp[:, :W4], in0=banks[2][:, :W4], scalar=ggla[:, 0:1], in1=rtmp[:, :W4],
                                                       op0=ALU.mult, op1=ALU.mult), [RB[2], res("rtmp"), res("ggla")], [res("dn")])
            OP("dve", lambda h: h.tensor_tensor(out=mixT[:, 4:8, s * TS:(s + 1) * TS], in0=otmp[:, :W4].rearrange("p (a t) -> p a t", a=4),
                                                in1=sg[:, :, s * TS:(s + 1) * TS], op=ALU.mult), [res("dn")] + [res(f"sg{i_}") for i_ in range(4)], [res(f"mixG{s}")])

        gla_batch()
        for s in range(nsub):
            g1(s)
            attention(s)
            g2(s)
            g3(s)
            att_B(s)
            g4(s)
            att_C(s)
            g5(s)

        if want_kout:
            kd, vd, sd = (ksam_d, vsam_d, ssam_d) if sample else (kp_d, vp_d, sp_d)
            OP("pe", lambda h: h.transpose(out=banks[7][:TS, 0:128], in_=k32T[:, :TS], identity=identf[:]), [res("k32T"), res("identf")], [RB[7]])
            OP("act", lambda h: h.copy(out=kout32[:TS, :], in_=banks[7][:TS, 0:128]), [RB[7]], [rko])
            out_toks.append(P.dma("pool", g_out, lambda h: h.dma_start(out=kd[:, :], in_=kout32[:TS, :]), reads=[rko]))
            out_toks.append(P.dma("pool", g_out, lambda h: h.dma_start(out=vd[:, :], in_=vout32[:TS, :]), reads=[rvo]))
            for hh in range(4):
                out_toks.append(P.dma("pool", g_out, lambda h, hh=hh: h.dma_start(out=sd[hh], in_=S[(hh % 2) * 64:(hh % 2 + 1) * 64, hh // 2, :]),
                                      reads=[res("S")]))
        if mode == "main":
            for kvh in range(2):
                for p in range(2):
                    OP("dve", lambda h, kvh=kvh, p=p: h.tensor_copy(out=KTz[kvh][p][:, 0:128], in_=KTz[kvh][p][:, MT:MT + 128]),
                       [res(f"KT{kvh}")], [res(f"KT{kvh}")])
            OP("dve", lambda h: h.tensor_copy(out=Vlo[:, 0, :, :], in_=Vlo[:, 4, :, :]), [res("V")], [res("V")])
            OP("dve", lambda h: h.tensor_copy(out=Vhi[:, 0, :, :], in_=Vhi[:, 4, :, :]), [res("V")], [res("V")])

        slots = acquire()
        for s in range(nsub):
            for half in range(2):
                bk = 4 + half
                for kc in range(8):
                    slot, rr_ = slots[kc]
                    OP("pe", lambda h, kc=kc, slot=slot, bk=bk, half=half, s=s: h.matmul(
                        banks[bk][:TS, :], lhsT=mixT[:, kc, s * TS:(s + 1) * TS], rhs=ring[:, slot, half * 512:(half + 1) * 512],
                        start=(kc == 0), stop=(kc == 7)), [rr_, res(f"mixA{s}"), res(f"mixG{s}")], [RB[bk]])
                OP("dve", lambda h, s=s, half=half, bk=bk: h.tensor_tensor(out=xt[:TS, s, half * 512:(half + 1) * 512],
                                                                          in0=banks[bk][:TS, :], in1=xt[:TS, s, half * 512:(half + 1) * 512],
                                                                          op=ALU.add), [RB[bk], R_xt[s]], [R_xt[s]])
        release()

        norm_T(4, nTb, R_nTb)
        nT, R_nT = nTb, R_nTb
        RnT = R_nT[:nsub]

        for G in range(4):
            if G == 2:
                yield "mid"
            ab = acti[0] % 2
            acti[0] += 1
            rA = res(f"actT{ab}")
            for j in range(8):
                (slot, rr_), = acquire()
                bk = (0, 1, 6, 7)[mmb[0] % 4]
                rb = mmb[0] % 2
                mmb[0] += 1
                for kc in range(8):
                    OP("pe", lambda h, kc=kc, slot=slot, bk=bk: h.matmul(banks[bk][:, :NT], lhsT=ring[:, slot, kc * 128:(kc + 1) * 128],
                                                                       rhs=nTb[:, kc, :NT], start=(kc == 0), stop=(kc == 7)),
                       [rr_] + RnT, [RB[bk]])
                release()
                OP("act", lambda h, bk=bk, rb=rb: h.activation(out=relu_t[rb][:, :NT], in_=banks[bk][:, :NT], func=AF.Relu),
                   [RB[bk]], [res(f"relu{rb}")])
                OP("act", lambda h, rb=rb, ab=ab, j=j: h.activation(out=actT[ab][:, j, :NT], in_=relu_t[rb][:, :NT], func=AF.Square),
                   [res(f"relu{rb}")], [rA])
            slots = acquire()
            for s in range(nsub):
                for half in range(2):
                    bk = 4 + half
                    for j in range(8):
                        slot, rr_ = slots[j]
                        OP("pe", lambda h, j=j, slot=slot, bk=bk, half=half, s=s, ab=ab: h.matmul(
                            banks[bk][:TS, :], lhsT=actT[ab][:, j, s * TS:(s + 1) * TS], rhs=ring[:, slot, half * 512:(half + 1) * 512],
                            start=(j == 0), stop=(j == 7)), [rr_, rA], [RB[bk]])
                    OP("dve", lambda h, s=s, half=half, bk=bk: h.tensor_tensor(out=xt[:TS, s, half * 512:(half + 1) * 512],
                                                                              in0=banks[bk][:TS, :], in1=xt[:TS, s, half * 512:(half + 1) * 512],
                                                                              op=ALU.add), [RB[bk], R_xt[s]], [R_xt[s]])
            release()

        dst = ys_d if sample else y_d
        o0 = 0 if sample else tok0 - npre * MT
        for s in range(nsub):
            y_toks.append(P.dma("pool", g_y[s], lambda h, s=s: h.dma_start(out=dst[o0 + s * TS:o0 + (s + 1) * TS, :], in_=xt[:TS, s, :]),
                                reads=[R_xt[s]], writes=[]))

    gens = []
    mi = 0
    for ti, (mode, tok0) in enumerate(tiles):
        if mode == "main":
            gens.append(run_tile(mode, tok0, ti % 2, first_main=(mi == 0), last_main=(mi == nmain - 1)))
            mi += 1
        else:
            gens.append(run_tile(mode, tok0, ti % 2))

    n_t = len(gens)
    pos = ["start"] * n_t
    nA_left = [(1 if tiles[i][0] in ("halo", "sample") else 4) for i in range(n_t)]

    def step1(i):
        pos[i] = next(gens[i], "end")
        if pos[i] == "normA":
            nA_left[i] -= 1
        return pos[i]

    def step_to(i, label):
        order = ["start", "load", "normA", "front", "mid", "end"]
        while order.index(pos[i]) < order.index(label) or (label == "normA" and pos[i] != "normA"):
            step1(i)
            if pos[i] == label:
                break
        assert pos[i] == label or (label == "front" and pos[i] == "front"), (i, pos[i], label)

    cur = [0]

    def hook(name):
        nx = cur[0] + 1
        if nx >= n_t:
            return
        if name == "Bsub":
            if pos[nx] in ("load", "normA") and nA_left[nx] > 0:
                step1(nx)
        elif name == "after_rev":
            while pos[nx] != "front":
                step1(nx)

    hook_holder[0] = hook
    for _ in range(11):
        prep_step()
    if n_t:
        step_to(0, "load")
        while pos[0] != "front":
            step1(0)
    if n_t > 1:
        step_to(1, "load")
    for ti in range(n_t):
        cur[0] = ti
        while pos[ti] != "mid":
            step1(ti)
        for _ in range(4):
            prep_step()
        if ti + 1 < n_t:
            while pos[ti + 1] != "front":
                step1(ti + 1)
        while pos[ti] != "end":
            step1(ti)
        if ti + 2 < n_t:
            while pos[ti + 2] != "load":
                step1(ti + 2)

    P.wait_all("pool", y_toks[-4:] + out_toks[-1:] + misc_toks[-1:])
    P.wait_all("sp", [("d", g, g.count) for g in g_ring if g.count > 0])
    assert gi[0] == len(stream)
    P.emit(block, sems)
    es.close()
    return nc


_CACHE = {}


def kernel(x_prompt, x_sample, cache_k, cache_v, state_gla, g_mix, w_in, w_alpha, b_alpha, g_q, g_k, sinks,
           g_gla_out, w_out, g_ffn, w_up, w_down):
    f = lambda a: np.ascontiguousarray(np.asarray(a, dtype=np.float32))
    x_prompt, x_sample = f(x_prompt), f(x_sample)
    B, T, _ = x_prompt.shape
    npre = NPRE
    if "nc" not in _CACHE:
        _CACHE["nc"] = build_program(npre)
    nc = _CACHE["nc"]
    common = {
        "g_mix": f(g_mix).reshape(D), "w_in": f(w_in).reshape(D, 2320), "w_alpha": f(w_alpha).reshape(16, 256),
        "b_alpha": f(b_alpha).reshape(1, 256), "g_q": f(g_q).reshape(64, 1), "g_k": f(g_k).reshape(64, 1),
        "sinks": f(sinks).reshape(1, 8), "g_gla": f(g_gla_out).reshape(128, 1), "w_out": f(w_out).reshape(D, D),
        "g_ffn": f(g_ffn).reshape(D), "w_up": f(w_up).reshape(D, 4096), "w_down": f(w_down).reshape(4096, D),
    }
    in_maps = []
    PRE = npre * MT
    for c in range(8):
        b, j = c // 4, c % 4
        start = j * SEG
        xin = np.zeros((PRE + SEG, D), np.float32)
        lo = max(0, start - PRE)
        xin[PRE - (start - lo):PRE + SEG] = x_prompt[b, lo:start + SEG]
        hbv = np.full((128, 1), 0.0 if j > 0 else -30000.0, np.float32)
        m = dict(common)
        m.update({"xin": xin, "xs": x_sample[c], "ck": f(cache_k)[0, c].reshape(128, 128), "cv": f(cache_v)[0, c].reshape(128, 128),
                  "st": f(state_gla)[0, c], "hb": hbv})
        in_maps.append(m)
    res = run_bass_kernel_spmd(nc, in_maps, core_ids=list(range(8)))
    r = res.results
    y_prompt = np.stack([np.concatenate([r[b * 4 + j]["y"] for j in range(4)], axis=0) for b in range(2)])
    y_sample = np.stack([r[c]["ys"] for c in range(8)])
    k_prompt = np.stack([r[b * 4 + 3]["kp"].reshape(128, 2, 64) for b in range(2)])[None]
    v_prompt = np.stack([r[b * 4 + 3]["vp"].reshape(128, 2, 64) for b in range(2)])[None]
    s_prompt = np.stack([r[b * 4 + 3]["sp"] for b in range(2)])[None]
    k_sample = np.stack([r[c]["ksam"].reshape(32, 2, 64) for c in range(8)])[None]
    v_sample = np.stack([r[c]["vsam"].reshape(32, 2, 64) for c in range(8)])[None]
    s_sample = np.stack([r[c]["ssam"] for c in range(8)])[None]
    return (y_prompt.astype(np.float32), y_sample.astype(np.float32), k_prompt.astype(np.float32), v_prompt.astype(np.float32),
            s_prompt.astype(np.float32), k_sample.astype(np.float32), v_sample.astype(np.float32), s_sample.astype(np.float32))
```
